# Optimizing a Trainium2 kernel written in Bass

```python
import jax, jax.numpy as jnp
from jax import lax
import numpy as np

D_MODEL = 2048
BATCH = 4
SEQ = 4096
DEPTH = 4

HEAD_DIM = 128
N_Q_HEADS = D_MODEL // HEAD_DIM
N_KV_HEADS = max(N_Q_HEADS // 4, 1)
GQA_GROUP = N_Q_HEADS // N_KV_HEADS
WINDOW = 128
BLOCK = 128
ROPE_DIM = HEAD_DIM // 4
ROPE_THETA = 500000.0
D_CONV = D_MODEL
CONV_WIDTH = 31
D_FF = ((8 * D_MODEL // 3 + 255) // 256) * 256
D_Q = N_Q_HEADS * HEAD_DIM
D_KV = N_KV_HEADS * HEAD_DIM
D_IN = D_Q + 2 * D_KV + 2 * D_CONV + 2 * D_MODEL
SPLITS = (D_Q, D_Q + D_KV, D_Q + 2 * D_KV, D_Q + 2 * D_KV + D_CONV,
          D_Q + 2 * D_KV + 2 * D_CONV, D_Q + 2 * D_KV + 2 * D_CONV + D_MODEL)
N_MOD = 6
DEEPNORM_ALPHA = (2.0 * DEPTH) ** 0.25
DEEPNORM_BETA = (8.0 * DEPTH) ** -0.25
LN_EPS = 1e-5
NEG_INF = -1e30

kernel_name = 'hybrid_swa_conformer_deepnorm_adaln'


def layer_norm(x, g, b):
    xf = x.astype(jnp.float32)
    mu = jnp.mean(xf, axis=-1, keepdims=True)
    xc = xf - mu
    var = jnp.mean(xc * xc, axis=-1, keepdims=True)
    y = xc * lax.rsqrt(var + LN_EPS) * g.astype(jnp.float32) + b.astype(jnp.float32)
    return y.astype(x.dtype)


def partial_rotary(t, cos, sin):
    half = ROPE_DIM // 2
    tf = t[..., :ROPE_DIM].astype(jnp.float32)
    t1, t2 = tf[..., :half], tf[..., half:]
    cs, sn = cos[None, :, None, :], sin[None, :, None, :]
    rot = jnp.concatenate([t1 * cs - t2 * sn, t2 * cs + t1 * sn], axis=-1).astype(t.dtype)
    return jnp.concatenate([rot, t[..., ROPE_DIM:]], axis=-1)


def _band(t, nb):
    B, _, H, D = t.shape
    tp = jnp.pad(t, ((0, 0), (BLOCK, BLOCK), (0, 0), (0, 0))).reshape(B, nb + 2, BLOCK, H, D)
    return jnp.concatenate([tp[:, :-2], tp[:, 1:-1], tp[:, 2:]], axis=2)


def window_gqa_with_sink(q, k, v, sink):
    B, S = q.shape[0], q.shape[1]
    nb = S // BLOCK
    qb = q.reshape(B, nb, BLOCK, N_KV_HEADS, GQA_GROUP, HEAD_DIM)
    kb, vb = _band(k, nb), _band(v, nb)
    s = jnp.einsum('bnqhgd,bnkhd->bnhgqk', qb, kb).astype(jnp.float32) * (HEAD_DIM ** -0.5)
    q_pos = jnp.arange(nb)[:, None] * BLOCK + jnp.arange(BLOCK)[None, :]
    k_pos = jnp.arange(nb)[:, None] * BLOCK - BLOCK + jnp.arange(3 * BLOCK)[None, :]
    valid = ((k_pos >= 0) & (k_pos < S))[:, None, :] & \
        (jnp.abs(k_pos[:, None, :] - q_pos[:, :, None]) <= WINDOW)
    s = jnp.where(valid[None, :, None, None], s, NEG_INF)
    sink_l = sink.astype(jnp.float32).reshape(N_KV_HEADS, GQA_GROUP)[None, None, :, :, None, None]
    m = jnp.maximum(jnp.max(s, axis=-1, keepdims=True), sink_l)
    p = jnp.exp(s - m)
    denom = jnp.sum(p, axis=-1, keepdims=True) + jnp.exp(sink_l - m)
    o = jnp.einsum('bnhgqk,bnkhd->bnqhgd', (p / denom).astype(v.dtype), vb)
    return o.reshape(B, S, N_Q_HEADS * HEAD_DIM)


def conformer_conv(glu_a, glu_b, w_dw, ln_g, ln_b):
    u = glu_a * jax.nn.sigmoid(glu_b)
    u = lax.conv_general_dilated(
        u, w_dw[:, None, :].astype(u.dtype), window_strides=(1,),
        padding=[(CONV_WIDTH // 2, CONV_WIDTH // 2)],
        dimension_numbers=('NWC', 'WIO', 'NWC'), feature_group_count=u.shape[-1])
    return jax.nn.silu(layer_norm(u, ln_g, ln_b))


def setup_inputs(seed: int = 0) -> dict:
    key = jax.random.key(seed)
    ks = jax.random.split(key, 18)
    L = DEPTH

    def nrm(k, shape, scale):
        return jax.random.normal(k, shape, jnp.float32) * scale

    return {
        'x': nrm(ks[0], (BATCH, SEQ, D_MODEL), 1.0),
        'c': nrm(ks[1], (BATCH, D_MODEL), 1.0),
        'w_ada': nrm(ks[2], (L, D_MODEL, N_MOD * D_MODEL), 0.3 * D_MODEL ** -0.5),
        'b_ada': nrm(ks[3], (L, N_MOD * D_MODEL), 0.02),
        'w_in': nrm(ks[4], (L, D_MODEL, D_IN), D_MODEL ** -0.5),
        'sink': nrm(ks[5], (L, N_Q_HEADS), 1.0),
        'w_dw': nrm(ks[6], (L, CONV_WIDTH, D_CONV), CONV_WIDTH ** -0.5),
        'conv_ln_g': 1.0 + nrm(ks[7], (L, D_CONV), 0.02),
        'conv_ln_b': nrm(ks[8], (L, D_CONV), 0.02),
        'w_oa': nrm(ks[9], (L, D_Q, D_MODEL), D_Q ** -0.5),
        'w_ob': nrm(ks[10], (L, D_CONV, D_MODEL), D_CONV ** -0.5),
        'w_out': nrm(ks[11], (L, D_MODEL, D_MODEL), DEEPNORM_BETA * D_MODEL ** -0.5),
        'ln1_g': 1.0 + nrm(ks[12], (L, D_MODEL), 0.02),
        'ln1_b': nrm(ks[13], (L, D_MODEL), 0.02),
        'w_gu': nrm(ks[14], (L, D_MODEL, 2 * D_FF), D_MODEL ** -0.5),
        'w_down': nrm(ks[15], (L, D_FF, D_MODEL), DEEPNORM_BETA * D_FF ** -0.5),
        'ln2_g': 1.0 + nrm(ks[16], (L, D_MODEL), 0.02),
        'ln2_b': nrm(ks[17], (L, D_MODEL), 0.02),
    }


def reference(x, c, w_ada, b_ada, w_in, sink, w_dw, conv_ln_g, conv_ln_b, w_oa, w_ob, w_out,
              ln1_g, ln1_b, w_gu, w_down, ln2_g, ln2_b):
    B, S, _ = x.shape
    pos = jnp.arange(S, dtype=jnp.float32)
    inv_freq = ROPE_THETA ** (-jnp.arange(0, ROPE_DIM, 2, dtype=jnp.float32) / ROPE_DIM)
    ang = pos[:, None] * inv_freq[None, :]
    cos, sin = jnp.cos(ang), jnp.sin(ang)
    c_act = jax.nn.silu(c)
    for l in range(DEPTH):
        mod = (c_act @ w_ada[l] + b_ada[l])[:, None, :]
        sh_a, sc_a, gt_a, sh_f, sc_f, gt_f = jnp.split(mod, N_MOD, axis=-1)
        h = x * (1 + sc_a) + sh_a
        q, k, v, glu_a, glu_b, g_a, g_b = jnp.split(h @ w_in[l], SPLITS, axis=-1)
        q = partial_rotary(q.reshape(B, S, N_Q_HEADS, HEAD_DIM), cos, sin)
        k = partial_rotary(k.reshape(B, S, N_KV_HEADS, HEAD_DIM), cos, sin)
        v = v.reshape(B, S, N_KV_HEADS, HEAD_DIM)
        y_a = window_gqa_with_sink(q, k, v, sink[l]) @ w_oa[l]
        y_b = conformer_conv(glu_a, glu_b, w_dw[l], conv_ln_g[l], conv_ln_b[l]) @ w_ob[l]
        merged = jax.nn.sigmoid(g_a) * y_a + jax.nn.sigmoid(g_b) * y_b
        x = layer_norm(DEEPNORM_ALPHA * x + (1 + gt_a) * (merged @ w_out[l]), ln1_g[l], ln1_b[l])
        h = x * (1 + sc_f) + sh_f
        gate, up = jnp.split(h @ w_gu[l], 2, axis=-1)
        ffn = (jax.nn.silu(gate) * up) @ w_down[l]
        x = layer_norm(DEEPNORM_ALPHA * x + (1 + gt_f) * ffn, ln2_g[l], ln2_b[l])
    return x
```

```python
import numpy as np
from contextlib import ExitStack
import concourse.bass as bass
import concourse.mybir as mybir
from concourse.bass_utils import run_bass_kernel_spmd

F32 = mybir.dt.float32
BF16 = mybir.dt.bfloat16
ALU = mybir.AluOpType
AF = mybir.ActivationFunctionType
AX = mybir.AxisListType

D = 2048
KC = 16
L = 4
NH = 16
NKV = 4
DFF = 5632
FC = 44
CW = 31
TMAX = 2560
NOWN = 2048
ALPHA = (2.0 * L) ** 0.25
EPS = 1e-5
NEG = -1e30
QSCALE = 128.0 ** -0.5


def t_in(l):
    return TMAX - 128 * l


def t_out(l):
    return TMAX - 128 * (l + 1)


def blocks(T, bs=512):
    return [(t0, min(bs, T - t0)) for t0 in range(0, T, bs)]


class Eng:
    def __init__(self, name):
        self.name = name
        self.ops = []
        self.sem = None
        self.cnt = 0
        self.waited = {}
        self.pend = []


class Buf:
    __slots__ = ("w", "r", "ds", "name")

    def __init__(self, name=""):
        self.w = None
        self.r = []
        self.ds = None
        self.name = name


class DSem:
    def __init__(self, h):
        self.h = h
        self.cnt = 0


class Arena:
    def __init__(self, ap, nbytes):
        self.ap = ap
        self.n = nbytes
        self.off = 0
        self.base = 0

    def reset(self):
        self.off = self.base

    def alloc(self, nbytes, dt=BF16):
        nbytes = (nbytes + 63) // 64 * 64
        assert self.off + nbytes <= self.n, (self.off, nbytes, self.n)
        a = self.ap[:, self.off // 2:(self.off + nbytes) // 2]
        self.off += nbytes
        if dt is F32:
            a = a.bitcast(F32)
        return a


class KB:
    def __init__(self, nc, stack, nds=48):
        self.nc = nc
        self.E = {n: Eng(n) for n in ("pe", "act", "dve", "pool", "sp")}
        for n, e in self.E.items():
            e.sem = stack.enter_context(nc.semaphore("es_" + n))
        self.dsems = [DSem(stack.enter_context(nc.semaphore("ds%d" % i))) for i in range(nds)]
        self.free_ds = list(self.dsems)
        self.bar = stack.enter_context(nc.semaphore("bar"))
        self.barcnt = 0
        self.ninstr = 0

    def _wait(self, eng, toks):
        for (sid, semh, val) in toks:
            if eng.waited.get(sid, 0) >= val:
                continue
            eng.waited[sid] = val
            eng.ops.append(lambda h, semh=semh, val=val: h.wait_ge(semh, val))

    def _deps(self, reads, writes):
        toks = []
        for b in reads:
            if b.w is not None:
                toks.append(b.w)
        for b in writes:
            if b.w is not None:
                toks.append(b.w)
            toks.extend(b.r)
        return toks

    def op(self, en, fn, reads=(), writes=(), inc=True):
        eng = self.E[en]
        self._wait(eng, self._deps(reads, writes))
        self.ninstr += 1
        if inc:
            eng.cnt += 1
            tok = (id(eng), eng.sem, eng.cnt)
            eng.ops.append(lambda h, fn=fn, sem=eng.sem: fn(h).then_inc(sem, 1))
            for (b, kind) in eng.pend:
                if kind == "r":
                    b.r.append(tok)
                else:
                    b.w = tok
                    b.r = []
            eng.pend = []
            for b in reads:
                b.r.append(tok)
            for b in writes:
                b.w = tok
                b.r = []
            return tok
        eng.ops.append(fn)
        for b in reads:
            eng.pend.append((b, "r"))
        for b in writes:
            eng.pend.append((b, "w"))
        return None

    def dma(self, en, out, in_, reads=(), writes=(), key=None, **kw):
        eng = self.E[en]
        self._wait(eng, self._deps(reads, writes))
        if key.ds is None:
            key.ds = self.free_ds.pop()
        ds = key.ds
        ds.cnt += 16
        tok = (id(ds), ds.h, ds.cnt)
        self.ninstr += 1
        eng.ops.append(lambda h, out=out, in_=in_, kw=kw, dh=ds.h: h.dma_start(out=out, in_=in_, **kw).then_inc(dh, 16))
        for b in reads:
            b.r.append(tok)
        for b in writes:
            b.w = tok
            b.r = []
        return tok

    def mm(self, bank, out, pairs, reads=(), transpose=False, final=True, first=True):
        n = len(pairs)
        for i, (a, b) in enumerate(pairs):
            last = (i == n - 1) and final
            if transpose:
                fn = (lambda h, out=out, a=a, b=b: h.transpose(out=out, in_=a, identity=b))
            else:
                fn = (lambda h, out=out, a=a, b=b, s=(i == 0), e=(i == n - 1):
                      h.matmul(out, a, b, start=s, stop=e))
            self.op("pe", fn, reads=reads if i == 0 else (), writes=[bank] if (i == 0 and first) else (), inc=last)

    def barrier(self):
        sp = self.E["sp"]
        toks = []
        for n, e in self.E.items():
            if n != "sp" and e.cnt > 0:
                toks.append((id(e), e.sem, e.cnt))
        for d in self.dsems:
            if d.cnt > 0:
                toks.append((id(d), d.h, d.cnt))
        self._wait(sp, toks)
        self.barcnt += 1
        bar = self.bar
        sp.ops.append(lambda h: h.nop().then_inc(bar, 1))
        for n, e in self.E.items():
            if n != "sp":
                self._wait(e, [(id(bar), bar, self.barcnt)])
        self.free_ds = list(self.dsems)


def newbufs(n, name=""):
    return [Buf(name + str(i)) for i in range(n)]


def release(bufs):
    for b in bufs:
        b.ds = None


def build_program(nlayers=L, debug=False):
    nc = bass.Bass("TRN2", target_bir_lowering=False)
    dk = "ExternalOutput" if debug else "Internal"

    def din(name, shape, dt=F32):
        return nc.dram_tensor(name, list(shape), dt, kind="ExternalInput").ap()

    def dscr(name, shape, dt=F32):
        return nc.dram_tensor(name, list(shape), dt, kind=dk).ap()

    xin = din("xin", [TMAX, D])
    cpp_d = din("cpp", [128, 16])
    ctab_d = din("ctab", [64, TMAX])
    stab_d = din("stab", [64, TMAX])
    wdw_d = din("wdw", [L, 128, KC * CW])
    wada_d = din("wada", [L * 24, 128, KC * 512])
    bada_d = din("bada", [L, 6 * D])
    badapp_d = din("badapp", [L, 128, 96])
    winfm_d = din("winfm", [L * 84, 128, KC * 128])
    wv_d = din("wv", [L, 128, KC * 512])
    sinkb_d = din("sinkb", [128, L * NH])
    clnpp_d = din("clnpp", [L, 128, 32])
    lnrows_d = din("lnrows", [L, 4, D])
    woa_d = din("woa", [L * 16, 128, KC * 128])
    wob_d = din("wob", [L * 16, 128, KC * 128])
    wout_d = din("wout", [L * 4, 128, KC * 512])
    wgu_d = din("wgu", [L * 88, 128, KC * 128])
    wdown_d = din("wdown", [L * 4, 128, FC * 512])
    ident_d = din("ident", [128, 128])
    mask_d = din("mask", [128, 384])
    y = nc.dram_tensor("y", [NOWN, D], F32, kind="ExternalOutput").ap()

    XA = dscr("XA", [TMAX, D])
    XB = dscr("XB", [TMAX, D])
    OTd = dscr("OTd", [D, TMAX], BF16)
    Cd = dscr("Cd", [D, TMAX], BF16)
    SGd = dscr("SGd", [2 * D, TMAX], BF16)
    MTd = dscr("MTd", [D, TMAX], BF16)
    Zd = dscr("Zd", [TMAX, D])
    Hd = dscr("Hd", [DFF, TMAX], BF16)
    Gd = dscr("Gd", [2 * L, 128, D])

    RA_BYTES = 81920
    RB_BYTES = 90112
    MS_BYTES = 38400

    with ExitStack() as stack:
        ec = stack.enter_context
        RAt = ec(nc.sbuf_tensor("RA", [128, RA_BYTES // 2], BF16))
        RBt = ec(nc.sbuf_tensor("RB", [128, RB_BYTES // 2], BF16))
        MSt = ec(nc.sbuf_tensor("MS", [128, MS_BYTES // 2], BF16))
        pst = [ec(nc.psum_tensor("ps%d" % i, [128, 512], F32)) for i in range(8)]
        kb = KB(nc, stack)
        RA = Arena(RAt, RA_BYTES)
        RB = Arena(RBt, RB_BYTES)
        MS = Arena(MSt, MS_BYTES)
        PS = [p[:, :] for p in pst]
        PB = newbufs(8, "bank")

        ident_f = MS.alloc(512, F32)
        ident_b = MS.alloc(256)
        onesm = MS.alloc(256)
        mask = MS.alloc(384 * 4, F32)
        sinkb = MS.alloc(L * NH * 4, F32)
        modpp = MS.alloc(L * 64 * 4, F32)
        cactf = MS.alloc(64, F32)
        cact = MS.alloc(64)
        wdw = MS.alloc(KC * CW * 4, F32)
        clnpp = MS.alloc(32 * 4, F32)
        small = MS.alloc(1024, F32)
        MS.base = MS.off

        cb_ = Buf("const")
        kb.dma("sp", ident_f, ident_d, writes=[cb_], key=cb_)
        cb2 = Buf("const2")
        kb.dma("sp", mask, mask_d, writes=[cb2], key=cb2)
        cb3 = Buf("const3")
        kb.dma("sp", sinkb, sinkb_d, writes=[cb3], key=cb3)
        cb4 = Buf("const4")
        kb.dma("sp", cactf[:, 0:16], cpp_d, writes=[cb4], key=cb4)
        kb.op("dve", lambda h: h.tensor_copy(out=ident_b, in_=ident_f), reads=[cb_])
        kb.op("dve", lambda h: h.memset(onesm, 1.0 / D))
        kb.op("act", lambda h: h.activation(out=cactf[:, 0:16], in_=cactf[:, 0:16], func=AF.Silu), reads=[cb4], writes=[cb4])
        kb.op("act", lambda h: h.activation(out=cact[:, 0:16], in_=cactf[:, 0:16], func=AF.Copy), reads=[cb4])
        kb.barrier()

        def hview(ar_ap, T):
            return ar_ap[:, 0:KC * T].rearrange("p (k t) -> p k t", t=T)

        def wdma(slot3, src2, buf):
            flat = slot3.rearrange("p k t -> p (k t)")
            n = flat.shape[1]
            if n > 2048:
                flat = flat.rearrange("p (a b) -> p a b", b=2048)
                src2 = src2.rearrange("p (a b) -> p a b", b=2048)
            kb.dma("pool", flat, src2, writes=[buf], key=buf)

        def phase_mod():
            RB.reset()
            MS.reset()
            crep = RB.alloc(KC * 128 * 2).rearrange("p (k t) -> p k t", t=128)
            zt = RB.alloc(128 * 4, F32)
            kb.op("dve", lambda h: h.memset(zt, 0.0))
            for kc in range(KC):
                kb.op("dve", lambda h, kc=kc: h.tensor_scalar(out=crep[:, kc, :], in0=zt, scalar1=cactf[:, kc:kc + 1],
                                                              scalar2=None, op0=ALU.add))
            wsl = [RB.alloc(KC * 512 * 2).rearrange("p (k t) -> p k t", t=512) for _ in range(2)]
            wb = newbufs(2, "mw")
            brow = [RB.alloc(512 * 4, F32) for _ in range(2)]
            bb = newbufs(2, "brow")
            gst = [RB.alloc(512 * 4, F32) for _ in range(2)]
            gb = newbufs(2, "gst")
            bpp = RB.alloc(96 * 4, F32)
            bpb = Buf("bpp")
            kb.barrier()
            it = 0
            ig = 0
            for l in range(nlayers):
                kb.dma("sp", bpp[:, 0:96], badapp_d[l], writes=[bpb], key=bpb)
                for ti in range(24):
                    v, cbk = ti // 4, ti % 4
                    s = it % 2
                    it += 1
                    wdma(wsl[s], wada_d[l * 24 + ti], wb[s])
                    bank = it % 2
                    if v in (2, 5):
                        g = ig % 2
                        ig += 1
                        kb.dma("sp", brow[g], bada_d[l, ti * 512:(ti + 1) * 512].partition_broadcast(128), writes=[bb[g]], key=bb[g])
                        kb.mm(PB[bank], PS[bank], [(crep[:, kc, :], wsl[s][:, kc, :]) for kc in range(KC)], reads=[wb[s]])
                        kb.op("dve", lambda h, g=g, bank=bank: h.scalar_tensor_tensor(out=gst[g], in0=PS[bank], scalar=1.0, in1=brow[g],
                                                                                   op0=ALU.add, op1=ALU.add),
                              reads=[PB[bank], bb[g]], writes=[gb[g]])
                        kb.dma("sp", Gd[l * 2 + (1 if v == 5 else 0), :, cbk * 512:(cbk + 1) * 512], gst[g], reads=[gb[g]], key=gb[g])
                    else:
                        vi = {0: 0, 1: 1, 3: 2, 4: 3}[v]
                        for j in range(4):
                            kb.mm(PB[bank], PS[bank][:, j:j + 1],
                                  [(wsl[s][:, kc, j * 128:(j + 1) * 128], cact[:, kc:kc + 1]) for kc in range(KC)],
                                  reads=[wb[s]], final=(j == 3), first=(j == 0))
                        col = l * 64 + vi * 16 + cbk * 4
                        kb.op("dve", lambda h, bank=bank, col=col, v=v, cbk=cbk: h.scalar_tensor_tensor(
                            out=modpp[:, col:col + 4], in0=PS[bank][:, 0:4], scalar=(1.0 if v in (1, 4) else 0.0),
                            in1=bpp[:, v * 16 + cbk * 4:v * 16 + cbk * 4 + 4], op0=ALU.add, op1=ALU.add),
                            reads=[PB[bank], bpb])
            kb.barrier()

        def emit_hT(src, srcbuf, i, hT, mod_sc, mod_sh):
            for q in range(4):
                bank = 4 + q
                for j in range(4):
                    kc = q * 4 + j
                    kb.mm(PB[bank], PS[bank][:, j * 128:(j + 1) * 128], [(src[:, kc * 128:(kc + 1) * 128], ident_f)],
                          reads=[srcbuf], transpose=True, final=(j == 3), first=(j == 0))
                for j in range(4):
                    kc = q * 4 + j
                    kb.op("act", lambda h, bank=bank, j=j, kc=kc: h.activation(
                        out=hT[:, kc, i * 128:(i + 1) * 128], in_=PS[bank][:, j * 128:(j + 1) * 128], func=AF.Identity,
                        scale=modpp[:, mod_sc + kc:mod_sc + kc + 1], bias=modpp[:, mod_sh + kc:mod_sh + kc + 1]),
                        reads=[PB[bank]])

        def phase_prep(l):
            T = t_in(l)
            RA.reset()
            RB.reset()
            hT = hview(RA.alloc(KC * T * 2), T)
            xs = [RB.alloc(D * 4, F32) for _ in range(3)]
            xb = newbufs(3, "xs")
            nt = T // 128
            for i in range(nt):
                s = i % 3
                kb.dma("sp", xs[s], xin[i * 128:(i + 1) * 128, :], writes=[xb[s]], key=xb[s])
                emit_hT(xs[s], xb[s], i, hT, l * 64 + 16, l * 64 + 0)
            kb.barrier()
            return hT

        def phase_v(l, hT):
            T = t_in(l)
            RB.reset()
            v_all = RB.alloc((TMAX // 128) * 512 * 2).rearrange("p (i d) -> p i d", d=512)
            wt = RB.alloc(KC * 512 * 2).rearrange("p (k t) -> p k t", t=512)
            wbuf = Buf("wv")
            vbase = RB.off
            wdma(wt, wv_d[l], wbuf)
            for i in range(T // 128):
                bank = i % 4
                kb.mm(PB[bank], PS[bank], [(hT[:, kc, i * 128:(i + 1) * 128], wt[:, kc, :]) for kc in range(KC)], reads=[wbuf])
                kb.op("act", lambda h, i=i, bank=bank: h.activation(out=v_all[:, i, :], in_=PS[bank], func=AF.Copy), reads=[PB[bank]])
            kb.barrier()
            return v_all, vbase - 16384

        def phase_qk_attn(l, hT, v_all, rb_base):
            Ti, To = t_in(l), t_out(l)
            nq = To // 128
            RB.off = rb_base
            qT = RB.alloc(4 * TMAX * 2).rearrange("p (j t) -> p j t", t=TMAX)
            kT = RB.alloc(TMAX * 2)
            ctab = RB.alloc(TMAX * 4, F32)
            stab = RB.alloc(TMAX * 4, F32)
            tb_ = Buf("tabs")
            kb.dma("sp", ctab[0:64, :], ctab_d, writes=[tb_], key=tb_)
            tb2 = Buf("tabs2")
            kb.dma("sp", stab[0:64, :], stab_d, writes=[tb2], key=tb2)
            base2 = RB.off
            for g in range(NKV):
                RB.off = base2
                MS.reset()
                wsl = [MS.alloc(KC * 128 * 2).rearrange("p (k t) -> p k t", t=128) for _ in range(3)]
                wb = newbufs(3, "wqk")
                qf = [RB.alloc(512 * 4, F32) for _ in range(2)]
                qfb = newbufs(2, "qf")
                tmp = [RB.alloc(512 * 4, F32) for _ in range(2)]
                tmb = newbufs(2, "tmp")
                r1 = [RB.alloc(512 * 4, F32) for _ in range(2)]
                r1b = newbufs(2, "r1")
                r2 = [RB.alloc(512 * 4, F32) for _ in range(2)]
                r2b = newbufs(2, "r2")
                it = 0
                for ci in range(5):
                    isk = ci == 4
                    tile = winfm_d[l * 84 + (16 + g if isk else 4 * g + ci)]
                    s = ci % 3
                    wdma(wsl[s], tile, wb[s])
                    dest = kT if isk else qT[:, ci, :]
                    sc = 1.0 if isk else QSCALE
                    for (t0, n) in blocks(Ti if isk else To):
                        bank = it % 4
                        u = it % 2
                        it += 1
                        kb.mm(PB[bank], PS[bank][:, 0:n], [(wsl[s][:, kc, :], hT[:, kc, t0:t0 + n]) for kc in range(KC)], reads=[wb[s]])
                        kb.op("act", lambda h, bank=bank, u=u, n=n, sc=sc: h.activation(out=qf[u][0:64, 0:n], in_=PS[bank][0:64, 0:n],
                                                                                      func=AF.Copy, scale=sc),
                              reads=[PB[bank]], writes=[qfb[u]])
                        kb.op("act", lambda h, bank=bank, n=n, sc=sc, dest=dest, t0=t0: h.activation(
                            out=dest[64:128, t0:t0 + n], in_=PS[bank][64:128, 0:n], func=AF.Copy, scale=sc), reads=[PB[bank]])
                        kb.op("dve", lambda h, u=u, n=n: h.tensor_copy(out=tmp[u][0:32, 0:n], in_=qf[u][32:64, 0:n]),
                              reads=[qfb[u]], writes=[tmb[u]])
                        kb.op("dve", lambda h, u=u, n=n: h.tensor_copy(out=tmp[u][32:64, 0:n], in_=qf[u][0:32, 0:n]),
                              reads=[qfb[u]], writes=[tmb[u]])
                        kb.op("dve", lambda h, u=u, n=n, t0=t0: h.tensor_tensor(out=r1[u][0:64, 0:n], in0=qf[u][0:64, 0:n],
                                                                              in1=ctab[0:64, t0:t0 + n], op=ALU.mult),
                              reads=[qfb[u], tb_], writes=[r1b[u]])
                        kb.op("dve", lambda h, u=u, n=n, t0=t0: h.tensor_tensor(out=r2[u][0:64, 0:n], in0=tmp[u][0:64, 0:n],
                                                                              in1=stab[0:64, t0:t0 + n], op=ALU.mult),
                              reads=[tmb[u], tb2], writes=[r2b[u]])
                        kb.op("dve", lambda h, u=u, n=n, t0=t0, dest=dest: h.tensor_tensor(out=dest[0:64, t0:t0 + n], in0=r1[u][0:64, 0:n],
                                                                                         in1=r2[u][0:64, 0:n], op=ALU.add),
                              reads=[r1b[u], r2b[u]])
                kb.barrier()
                RB.off = base2
                sm = [RB.alloc(384 * 4, F32) for _ in range(2)]
                smb = newbufs(2, "sm")
                Pm = [RB.alloc(384 * 2) for _ in range(2)]
                Pb_ = newbufs(2, "P")
                Pn = [RB.alloc(384 * 2) for _ in range(2)]
                Pnb = newbufs(2, "Pn")
                PTs = [RB.alloc(384 * 2) for _ in range(2)]
                PTb = newbufs(2, "PTs")
                st = [RB.alloc(64, F32) for _ in range(2)]
                stb = newbufs(2, "st")
                OTst = [RB.alloc(4 * 128 * 2).rearrange("p (j t) -> p j t", t=128) for _ in range(2)]
                OTb = newbufs(2, "OTst")
                OTv = OTd.rearrange("(h p) t -> p h t", p=128)
                items = [(n, j) for n in range(nq) for j in range(4)]
                NI = len(items)

                def stA(s):
                    n, j = items[s]
                    u = s % 2
                    hcol = l * NH + 4 * g + j
                    nk = 2 if n == 0 else 3
                    ks = max(n - 1, 0) * 128
                    nkeys = nk * 128
                    mo = 128 if n == 0 else 0
                    bank = u
                    kb.mm(PB[bank], PS[bank][:, 0:nkeys], [(qT[:, j, n * 128:(n + 1) * 128], kT[:, ks:ks + nkeys])])
                    kb.op("dve", lambda h: h.tensor_tensor(out=sm[u][:, 0:nkeys], in0=PS[bank][:, 0:nkeys], in1=mask[:, mo:mo + nkeys], op=ALU.add),
                          reads=[PB[bank]], writes=[smb[u]])
                    kb.op("dve", lambda h: h.reduce_max(out=st[u][:, 0:1], in_=sm[u][:, 0:nkeys], axis=AX.X), reads=[smb[u]], writes=[stb[u]])
                    kb.op("dve", lambda h: h.tensor_scalar(out=st[u][:, 1:2], in0=st[u][:, 0:1], scalar1=sinkb[:, hcol:hcol + 1], scalar2=-1.0,
                                                           op0=ALU.max, op1=ALU.mult), reads=[stb[u]], writes=[stb[u]])
                    kb.op("act", lambda h: h.activation(out=Pm[u][:, 0:nkeys], in_=sm[u][:, 0:nkeys], func=AF.Exp, bias=st[u][:, 1:2], scale=1.0,
                                                        accum_out=st[u][:, 2:3]), reads=[smb[u], stb[u]], writes=[Pb_[u], stb[u]])
                    kb.op("act", lambda h: h.activation(out=st[u][:, 3:4], in_=sinkb[:, hcol:hcol + 1], func=AF.Exp, bias=st[u][:, 1:2], scale=1.0),
                          reads=[stb[u]], writes=[stb[u]])
                    kb.op("dve", lambda h: h.tensor_tensor(out=st[u][:, 4:5], in0=st[u][:, 2:3], in1=st[u][:, 3:4], op=ALU.add),
                          reads=[stb[u]], writes=[stb[u]])
                    kb.op("dve", lambda h: h.reciprocal(out=st[u][:, 5:6], in_=st[u][:, 4:5]), reads=[stb[u]], writes=[stb[u]])
                    kb.op("dve", lambda h: h.tensor_scalar(out=Pn[u][:, 0:nkeys], in0=Pm[u][:, 0:nkeys], scalar1=st[u][:, 5:6], scalar2=None,
                                                           op0=ALU.mult), reads=[Pb_[u], stb[u]], writes=[Pnb[u]])

                def stB(s):
                    n, j = items[s]
                    u = s % 2
                    nk = 2 if n == 0 else 3
                    bank = 2 + u
                    pv = PS[bank].bitcast(BF16)
                    for kbk in range(nk):
                        kb.mm(PB[bank], pv[:, kbk * 128:(kbk + 1) * 128], [(Pn[u][:, kbk * 128:(kbk + 1) * 128], ident_b)],
                              reads=[Pnb[u]], transpose=True, final=(kbk == nk - 1), first=(kbk == 0))
                    kb.op("act", lambda h: h.activation(out=PTs[u][:, 0:nk * 128], in_=pv[:, 0:nk * 128], func=AF.Copy),
                          reads=[PB[bank]], writes=[PTb[u]])

                def stC(s):
                    n, j = items[s]
                    u = s % 2
                    nk = 2 if n == 0 else 3
                    kb0 = max(n - 1, 0)
                    bank = 4 + u
                    o = n % 2
                    kb.mm(PB[bank], PS[bank][:, 0:128],
                          [(v_all[:, kb0 + kbk, g * 128:(g + 1) * 128], PTs[u][:, kbk * 128:(kbk + 1) * 128]) for kbk in range(nk)],
                          reads=[PTb[u]])
                    kb.op("act", lambda h: h.activation(out=OTst[o][:, j, :], in_=PS[bank][:, 0:128], func=AF.Copy),
                          reads=[PB[bank]], writes=[OTb[o]])
                    if j == 3:
                        kb.dma("sp", OTv[:, 4 * g:4 * g + 4, n * 128:(n + 1) * 128], OTst[o], reads=[OTb[o]], key=OTb[o])

                for s in range(NI + 2):
                    if s < NI:
                        stA(s)
                    if 0 <= s - 1 < NI:
                        stB(s - 1)
                    if 0 <= s - 2 < NI:
                        stC(s - 2)
                kb.barrier()

        def phase_conv_gates(l, hT):
            Ti, To = t_in(l), t_out(l)
            RB.reset()
            MS.reset()
            wsl = [RB.alloc(KC * 128 * 2).rearrange("p (k t) -> p k t", t=128) for _ in range(8)]
            wb = newbufs(8, "wcg")
            ub = [RB.alloc((15 + TMAX) * 4, F32) for _ in range(2)]
            ubb = newbufs(2, "u")
            acc = RB.alloc(TMAX * 4, F32)
            accb = Buf("acc")
            Cst = [RB.alloc(TMAX * 2) for _ in range(2)]
            Cstb = newbufs(2, "Cst")
            sg = [RB.alloc(512 * 4, F32) for _ in range(2)]
            sgb = newbufs(2, "sg")
            gs = [RB.alloc(512 * 2) for _ in range(4)]
            gsb = newbufs(4, "gs")
            wdb = Buf("wdw")
            kb.dma("sp", wdw, wdw_d[l], writes=[wdb], key=wdb)
            for u in range(2):
                kb.op("dve", lambda h, u=u: h.memset(ub[u][:, 0:15], 0.0), writes=[ubb[u]])
            ia = 0
            ig = 0
            for c in range(KC):
                base = l * 84
                tiles = [base + 20 + c, base + 36 + c, base + 52 + c, base + 68 + c]
                ws = [(c % 2) * 4 + q for q in range(4)]
                for q in range(4):
                    wdma(wsl[ws[q]], winfm_d[tiles[q]], wb[ws[q]])
                uu = c % 2
                for (t0, n) in blocks(Ti):
                    ba, bb = (ia % 2) * 2, (ia % 2) * 2 + 1
                    s = ia % 2
                    ia += 1
                    kb.mm(PB[ba], PS[ba][:, 0:n], [(wsl[ws[0]][:, kc, :], hT[:, kc, t0:t0 + n]) for kc in range(KC)], reads=[wb[ws[0]]])
                    kb.mm(PB[bb], PS[bb][:, 0:n], [(wsl[ws[1]][:, kc, :], hT[:, kc, t0:t0 + n]) for kc in range(KC)], reads=[wb[ws[1]]])
                    kb.op("act", lambda h, bb=bb, s=s, n=n: h.activation(out=sg[s][:, 0:n], in_=PS[bb][:, 0:n], func=AF.Sigmoid),
                          reads=[PB[bb]], writes=[sgb[s]])
                    kb.op("dve", lambda h, ba=ba, s=s, n=n, t0=t0, uu=uu: h.tensor_tensor(out=ub[uu][:, 15 + t0:15 + t0 + n], in0=PS[ba][:, 0:n],
                                                                                        in1=sg[s][:, 0:n], op=ALU.mult),
                          reads=[PB[ba], sgb[s]], writes=[ubb[uu]])
                for which in range(2):
                    for (t0, n) in blocks(To):
                        bank = 4 + ig % 4
                        s = ig % 4
                        ig += 1
                        kb.mm(PB[bank], PS[bank][:, 0:n], [(wsl[ws[2 + which]][:, kc, :], hT[:, kc, t0:t0 + n]) for kc in range(KC)],
                              reads=[wb[ws[2 + which]]])
                        kb.op("act", lambda h, bank=bank, s=s, n=n: h.activation(out=gs[s][:, 0:n], in_=PS[bank][:, 0:n], func=AF.Sigmoid),
                              reads=[PB[bank]], writes=[gsb[s]])
                        kb.dma("sp", SGd[which * D + c * 128:which * D + (c + 1) * 128, t0:t0 + n], gs[s][:, 0:n], reads=[gsb[s]], key=gsb[s])
                cs = c % 2
                for k in range(CW):
                    wk = wdw[:, c * CW + k:c * CW + k + 1]
                    if k == 0:
                        kb.op("dve", lambda h, uu=uu, wk=wk: h.tensor_scalar(out=acc[:, 0:To], in0=ub[uu][:, 0:To], scalar1=wk, scalar2=None, op0=ALU.mult),
                              reads=[ubb[uu], wdb], writes=[accb])
                    elif k < CW - 1:
                        kb.op("dve", lambda h, uu=uu, wk=wk, k=k: h.scalar_tensor_tensor(out=acc[:, 0:To], in0=ub[uu][:, k:k + To], scalar=wk,
                                                                                         in1=acc[:, 0:To], op0=ALU.mult, op1=ALU.add),
                              reads=[ubb[uu], accb], writes=[accb])
                    else:
                        kb.op("dve", lambda h, uu=uu, wk=wk, k=k, cs=cs: h.scalar_tensor_tensor(out=Cst[cs][:, 0:To], in0=ub[uu][:, k:k + To], scalar=wk,
                                                                                                in1=acc[:, 0:To], op0=ALU.mult, op1=ALU.add),
                              reads=[ubb[uu], accb], writes=[Cstb[cs]])
                kb.dma("sp", Cd[c * 128:(c + 1) * 128, 0:To], Cst[cs][:, 0:To], reads=[Cstb[cs]], key=Cstb[cs])
            kb.barrier()

        def phase_convln(l):
            To = t_out(l)
            RA.reset()
            RB.reset()
            cbT = hview(RA.alloc(KC * To * 2), To)
            Cb = [RB.alloc(KC * 512 * 2).rearrange("p (k t) -> p k t", t=512) for _ in range(2)]
            Cbb = newbufs(2, "Cb")
            SQ = RB.alloc(KC * 512 * 2).rearrange("p (k t) -> p k t", t=512)
            SQb = Buf("SQ")
            msb = RB.alloc(512 * 4, F32)
            m2 = RB.alloc(512 * 4, F32)
            rstd = RB.alloc(512 * 4, F32)
            stb = Buf("lnst")
            t1 = [RB.alloc(512 * 4, F32) for _ in range(2)]
            t1b = newbufs(2, "t1")
            t2 = [RB.alloc(512 * 4, F32) for _ in range(2)]
            t2b = newbufs(2, "t2")
            pb = Buf("clnpp")
            kb.dma("sp", clnpp, clnpp_d[l], writes=[pb], key=pb)
            Cv = Cd.rearrange("(j p) t -> p j t", p=128)
            bl = blocks(To)
            for bi in range(len(bl) + 1):
                if bi < len(bl):
                    t0, n = bl[bi]
                    kb.dma("sp", Cb[bi % 2][:, :, 0:n], Cv[:, :, t0:t0 + n], writes=[Cbb[bi % 2]], key=Cbb[bi % 2])
                if bi == 0:
                    continue
                t0, n = bl[bi - 1]
                s = (bi - 1) % 2
                kb.op("act", lambda h, s=s, n=n: h.activation(out=SQ[:, :, 0:n], in_=Cb[s][:, :, 0:n], func=AF.Square),
                      reads=[Cbb[s]], writes=[SQb])
                kb.mm(PB[0], PS[0][:, 0:n], [(onesm, Cb[s][:, j, 0:n]) for j in range(KC)], reads=[Cbb[s]])
                kb.mm(PB[1], PS[1][:, 0:n], [(onesm, SQ[:, j, 0:n]) for j in range(KC)], reads=[SQb])
                kb.op("act", lambda h, n=n: h.activation(out=msb[:, 0:n], in_=PS[0][:, 0:n], func=AF.Copy), reads=[PB[0]], writes=[stb])
                kb.op("dve", lambda h, n=n: h.tensor_tensor(out=m2[:, 0:n], in0=msb[:, 0:n], in1=msb[:, 0:n], op=ALU.mult), reads=[stb], writes=[stb])
                kb.op("dve", lambda h, n=n: h.tensor_tensor(out=m2[:, 0:n], in0=PS[1][:, 0:n], in1=m2[:, 0:n], op=ALU.subtract),
                      reads=[PB[1], stb], writes=[stb])
                kb.op("dve", lambda h, n=n: h.tensor_scalar(out=m2[:, 0:n], in0=m2[:, 0:n], scalar1=0.0, scalar2=EPS, op0=ALU.max, op1=ALU.add),
                      reads=[stb], writes=[stb])
                kb.op("act", lambda h, n=n: h.activation(out=m2[:, 0:n], in_=m2[:, 0:n], func=AF.Sqrt), reads=[stb], writes=[stb])
                kb.op("dve", lambda h, n=n: h.reciprocal(out=rstd[:, 0:n], in_=m2[:, 0:n]), reads=[stb], writes=[stb])
                for j in range(KC):
                    q = j % 2
                    kb.op("dve", lambda h, s=s, j=j, q=q, n=n: h.tensor_tensor(out=t1[q][:, 0:n], in0=Cb[s][:, j, 0:n], in1=msb[:, 0:n], op=ALU.subtract),
                          reads=[Cbb[s], stb], writes=[t1b[q]])
                    kb.op("dve", lambda h, q=q, n=n: h.tensor_tensor(out=t2[q][:, 0:n], in0=t1[q][:, 0:n], in1=rstd[:, 0:n], op=ALU.mult),
                          reads=[t1b[q], stb], writes=[t2b[q]])
                    kb.op("act", lambda h, j=j, q=q, n=n, t0=t0: h.activation(out=cbT[:, j, t0:t0 + n], in_=t2[q][:, 0:n], func=AF.Silu,
                                                                            scale=clnpp[:, j:j + 1], bias=clnpp[:, 16 + j:17 + j]),
                          reads=[t2b[q], pb])
            kb.barrier()
            return cbT

        def phase_merge(l, cbT):
            To = t_out(l)
            RB.reset()
            MS.reset()
            OT = hview(RB.alloc(KC * To * 2), To)
            otb = newbufs(4, "OTld")
            OTv = OTd.rearrange("(h p) t -> p h t", p=128)
            for q in range(4):
                kb.dma("sp", OT[:, 4 * q:4 * q + 4, :], OTv[:, 4 * q:4 * q + 4, 0:To], writes=[otb[q]], key=otb[q])
            sga = [RB.alloc(512 * 2) for _ in range(2)]
            sgab = newbufs(2, "sga")
            sgb_ = [RB.alloc(512 * 2) for _ in range(2)]
            sgbb = newbufs(2, "sgb")
            tA = [MS.alloc(512 * 4, F32) for _ in range(2)]
            tAb = newbufs(2, "tA")
            tB = [MS.alloc(512 * 4, F32) for _ in range(2)]
            tBb = newbufs(2, "tB")
            mst = [MS.alloc(512 * 2) for _ in range(2)]
            mstb = newbufs(2, "mst")
            wsl = [MS.alloc(KC * 128 * 2).rearrange("p (k t) -> p k t", t=128) for _ in range(4)]
            wb = newbufs(4, "wmg")
            items = [(c, t0, n) for c in range(KC) for (t0, n) in blocks(To)]
            NI = len(items)

            def loads(s):
                c, t0, n = items[s]
                u = s % 2
                if t0 == 0:
                    w0, w1 = (c % 2) * 2, (c % 2) * 2 + 1
                    wdma(wsl[w0], woa_d[l * 16 + c], wb[w0])
                    wdma(wsl[w1], wob_d[l * 16 + c], wb[w1])
                kb.dma("sp", sga[u][:, 0:n], SGd[c * 128:(c + 1) * 128, t0:t0 + n], writes=[sgab[u]], key=sgab[u])
                kb.dma("sp", sgb_[u][:, 0:n], SGd[D + c * 128:D + (c + 1) * 128, t0:t0 + n], writes=[sgbb[u]], key=sgbb[u])

            def comp(s):
                c, t0, n = items[s]
                u = s % 2
                w0, w1 = (c % 2) * 2, (c % 2) * 2 + 1
                ba, bb = u * 2, u * 2 + 1
                kb.mm(PB[ba], PS[ba][:, 0:n], [(wsl[w0][:, kc, :], OT[:, kc, t0:t0 + n]) for kc in range(KC)], reads=[wb[w0]] + otb)
                kb.mm(PB[bb], PS[bb][:, 0:n], [(wsl[w1][:, kc, :], cbT[:, kc, t0:t0 + n]) for kc in range(KC)], reads=[wb[w1]])
                kb.op("dve", lambda h: h.tensor_tensor(out=tA[u][:, 0:n], in0=PS[ba][:, 0:n], in1=sga[u][:, 0:n], op=ALU.mult),
                      reads=[PB[ba], sgab[u]], writes=[tAb[u]])
                kb.op("dve", lambda h: h.tensor_tensor(out=tB[u][:, 0:n], in0=PS[bb][:, 0:n], in1=sgb_[u][:, 0:n], op=ALU.mult),
                      reads=[PB[bb], sgbb[u]], writes=[tBb[u]])
                kb.op("dve", lambda h: h.tensor_tensor(out=mst[u][:, 0:n], in0=tA[u][:, 0:n], in1=tB[u][:, 0:n], op=ALU.add),
                      reads=[tAb[u], tBb[u]], writes=[mstb[u]])
                kb.dma("sp", MTd[c * 128:(c + 1) * 128, t0:t0 + n], mst[u][:, 0:n], reads=[mstb[u]], key=mstb[u])

            for s in range(NI + 1):
                if s < NI:
                    loads(s)
                if s >= 1:
                    comp(s - 1)
            kb.barrier()

        def gemm_tm(l, To, A_view, Abufs_for_tile, wtile_fn, nkc, wsl, wb, zst, zstb, tiles):
            for nb in range(4):
                s = nb % 2
                wdma(wsl[s], wtile_fn(nb), wb[s])
                for (i, li) in tiles:
                    gemm_tm.cnt += 1
                    bank = gemm_tm.cnt % 4
                    z = gemm_tm.cnt % 3
                    kb.mm(PB[bank], PS[bank], [(A_view[:, kc, li * 128:(li + 1) * 128], wsl[s][:, kc, :]) for kc in range(nkc)],
                          reads=[wb[s]] + Abufs_for_tile)
                    kb.op("act", lambda h, bank=bank, z=z: h.activation(out=zst[z], in_=PS[bank], func=AF.Copy), reads=[PB[bank]], writes=[zstb[z]])
                    kb.dma("sp", Zd[i * 128:(i + 1) * 128, nb * 512:(nb + 1) * 512], zst[z], reads=[zstb[z]], key=zstb[z])
        gemm_tm.cnt = 0

        def phase_wout(l):
            To = t_out(l)
            RA.reset()
            RB.reset()
            MS.reset()
            MT = hview(RA.alloc(KC * To * 2), To)
            mtb = newbufs(4, "MTld")
            MTv = MTd.rearrange("(c p) t -> p c t", p=128)
            for q in range(4):
                kb.dma("sp", MT[:, 4 * q:4 * q + 4, :], MTv[:, 4 * q:4 * q + 4, 0:To], writes=[mtb[q]], key=mtb[q])
            wsl = [RB.alloc(KC * 512 * 2).rearrange("p (k t) -> p k t", t=512) for _ in range(2)]
            wb = newbufs(2, "wout")
            zst = [MS.alloc(512 * 4, F32) for _ in range(3)]
            zstb = newbufs(3, "zst")
            gemm_tm(l, To, MT, mtb, lambda nb: wout_d[l * 4 + nb], KC, wsl, wb, zst, zstb,
                    [(i, i) for i in range(To // 128)])
            kb.barrier()

        def phase_ln(l, which, xsrc, xdst, hmods, final=False):
            To = t_out(l)
            nt = (NOWN // 128) if final else (To // 128)
            RA.reset()
            RB.reset()
            MS.reset()
            hT = None if final else hview(RA.alloc(KC * To * 2), To)
            zs = [RB.alloc(D * 4, F32) for _ in range(2)]
            zb = newbufs(2, "zs")
            xs = [RB.alloc(D * 4, F32) for _ in range(2)]
            xb = newbufs(2, "xs")
            G = RB.alloc(D * 4, F32)
            lng = RB.alloc(D * 4, F32)
            lnb = RB.alloc(D * 4, F32)
            tA = RB.alloc(D * 4, F32)
            tAb = Buf("tA")
            tB = RB.alloc(D * 4, F32)
            tBb = Buf("tB")
            xo = [RB.alloc(D * 4, F32) for _ in range(2)]
            xob = newbufs(2, "xo")
            stt = MS.alloc(32 * 4, F32)
            sttb = Buf("lnstat")
            gb_, lgb, lbb = Buf("G"), Buf("lng"), Buf("lnb")
            kb.dma("sp", G, Gd[l * 2 + which], writes=[gb_], key=gb_)
            kb.dma("sp", lng, lnrows_d[l, 2 * which, :].partition_broadcast(128), writes=[lgb], key=lgb)
            kb.dma("sp", lnb, lnrows_d[l, 2 * which + 1, :].partition_broadcast(128), writes=[lbb], key=lbb)

            def loads(i):
                s = i % 2
                kb.dma("sp", zs[s], Zd[i * 128:(i + 1) * 128, :], writes=[zb[s]], key=zb[s])
                kb.dma("sp", xs[s], xsrc[i * 128:(i + 1) * 128, :], writes=[xb[s]], key=xb[s])

            def comp(i):
                s = i % 2
                kb.op("dve", lambda h: h.tensor_tensor(out=tA, in0=zs[s], in1=G, op=ALU.mult), reads=[zb[s], gb_], writes=[tAb])
                kb.op("dve", lambda h: h.scalar_tensor_tensor(out=tB, in0=xs[s], scalar=ALPHA, in1=tA, op0=ALU.mult, op1=ALU.add),
                      reads=[xb[s], tAb], writes=[tBb])
                for q in range(4):
                    kb.op("dve", lambda h, q=q: h.bn_stats(out=stt[:, q * 6:(q + 1) * 6], in_=tB[:, q * 512:(q + 1) * 512]),
                          reads=[tBb], writes=[sttb])
                kb.op("dve", lambda h: h.bn_aggr(out=stt[:, 24:26], in_=stt[:, 0:24]), reads=[sttb], writes=[sttb])
                kb.op("dve", lambda h: h.tensor_scalar(out=stt[:, 26:27], in0=stt[:, 25:26], scalar1=0.0, scalar2=EPS, op0=ALU.max, op1=ALU.add),
                      reads=[sttb], writes=[sttb])
                kb.op("act", lambda h: h.activation(out=stt[:, 27:28], in_=stt[:, 26:27], func=AF.Sqrt), reads=[sttb], writes=[sttb])
                kb.op("dve", lambda h: h.reciprocal(out=stt[:, 28:29], in_=stt[:, 27:28]), reads=[sttb], writes=[sttb])
                kb.op("dve", lambda h: h.tensor_scalar(out=tA, in0=tB, scalar1=stt[:, 24:25], scalar2=stt[:, 28:29], op0=ALU.subtract, op1=ALU.mult),
                      reads=[tBb, sttb], writes=[tAb])
                kb.op("dve", lambda h: h.tensor_tensor(out=tB, in0=tA, in1=lng, op=ALU.mult), reads=[tAb, lgb], writes=[tBb])
                kb.op("dve", lambda h: h.tensor_tensor(out=xo[s], in0=tB, in1=lnb, op=ALU.add), reads=[tBb, lbb], writes=[xob[s]])
                kb.dma("sp", xdst[i * 128:(i + 1) * 128, :], xo[s], reads=[xob[s]], key=xob[s])
                if not final:
                    emit_hT(xo[s], xob[s], i, hT, hmods[0], hmods[1])

            for i in range(nt + 1):
                if i < nt:
                    loads(i)
                if i >= 1:
                    comp(i - 1)
            kb.barrier()
            return hT

        def phase_ffn_up(l, hT2):
            To = t_out(l)
            RB.reset()
            MS.reset()
            wsl = [MS.alloc(KC * 128 * 2).rearrange("p (k t) -> p k t", t=128) for _ in range(4)]
            wb = newbufs(4, "wgu")
            sg = [RB.alloc(512 * 4, F32) for _ in range(2)]
            sgb = newbufs(2, "sgf")
            hst = [RB.alloc(TMAX * 2) for _ in range(2)]
            hstb = newbufs(2, "hst")
            it = 0
            for f in range(FC):
                w0, w1 = (f % 2) * 2, (f % 2) * 2 + 1
                wdma(wsl[w0], wgu_d[l * 88 + f], wb[w0])
                wdma(wsl[w1], wgu_d[l * 88 + FC + f], wb[w1])
                hs = f % 2
                for (t0, n) in blocks(To):
                    u = it % 2
                    bg, bu = u * 2, u * 2 + 1
                    it += 1
                    kb.mm(PB[bg], PS[bg][:, 0:n], [(wsl[w0][:, kc, :], hT2[:, kc, t0:t0 + n]) for kc in range(KC)], reads=[wb[w0]])
                    kb.mm(PB[bu], PS[bu][:, 0:n], [(wsl[w1][:, kc, :], hT2[:, kc, t0:t0 + n]) for kc in range(KC)], reads=[wb[w1]])
                    kb.op("act", lambda h, bg=bg, u=u, n=n: h.activation(out=sg[u][:, 0:n], in_=PS[bg][:, 0:n], func=AF.Silu),
                          reads=[PB[bg]], writes=[sgb[u]])
                    kb.op("dve", lambda h, bu=bu, u=u, n=n, t0=t0, hs=hs: h.tensor_tensor(out=hst[hs][:, t0:t0 + n], in0=PS[bu][:, 0:n],
                                                                                        in1=sg[u][:, 0:n], op=ALU.mult),
                          reads=[PB[bu], sgb[u]], writes=[hstb[hs]])
                kb.dma("sp", Hd[f * 128:(f + 1) * 128, 0:To], hst[hs][:, 0:To], reads=[hstb[hs]], key=hstb[hs])
            kb.barrier()

        def phase_ffn_down(l):
            To = t_out(l)
            nt = To // 128
            RA.reset()
            RB.reset()
            MS.reset()
            nsb = 3
            sizes = [nt // nsb + (1 if r < nt % nsb else 0) for r in range(nsb)]
            mx = max(sizes)
            A = RA.alloc(FC * mx * 128 * 2).rearrange("p (f t) -> p f t", t=mx * 128)
            ab = newbufs(4, "Ald")
            wsl = [RB.alloc(FC * 512 * 2).rearrange("p (k t) -> p k t", t=512) for _ in range(2)]
            wb = newbufs(2, "wdn")
            zst = [MS.alloc(512 * 4, F32) for _ in range(3)]
            zstb = newbufs(3, "zst")
            Hv = Hd.rearrange("(f p) t -> p f t", p=128)
            i0 = 0
            for sbi in range(nsb):
                ns = sizes[sbi]
                for q in range(4):
                    kb.dma("sp", A[:, 11 * q:11 * q + 11, 0:ns * 128], Hv[:, 11 * q:11 * q + 11, i0 * 128:(i0 + ns) * 128],
                           writes=[ab[q]], key=ab[q])
                gemm_tm(l, To, A, ab, lambda nb: wdown_d[l * 4 + nb], FC, wsl, wb, zst, zstb,
                        [(i0 + li, li) for li in range(ns)])
                i0 += ns
            kb.barrier()

        phase_mod()
        hT = phase_prep(0)
        for l in range(nlayers):
            v_all, rb_base = phase_v(l, hT)
            phase_qk_attn(l, hT, v_all, rb_base)
            phase_conv_gates(l, hT)
            cbT = phase_convln(l)
            phase_merge(l, cbT)
            phase_wout(l)
            x_in = xin if l == 0 else XB
            hT2 = phase_ln(l, 0, x_in, XA, (l * 64 + 48, l * 64 + 32))
            phase_ffn_up(l, hT2)
            phase_ffn_down(l)
            last = (l == nlayers - 1)
            if last:
                ydst = y if nlayers == L else XB
                phase_ln(l, 1, XA, ydst, None, final=True)
            else:
                hT = phase_ln(l, 1, XA, XB, ((l + 1) * 64 + 16, (l + 1) * 64 + 0))

        with nc.Block() as block:
            def mk(name):
                def run(h):
                    for f in kb.E[name].ops:
                        f(h)
                return run
            block.sync(mk("sp"))
            block.tensor(mk("pe"))
            block.scalar(mk("act"))
            block.vector(mk("dve"))
            block.gpsimd(mk("pool"))
        stats = {n: len(e.ops) for n, e in kb.E.items()}
    return nc, stats


def _fm_tiles(W):
    K, N = W.shape
    return np.ascontiguousarray(W.reshape(K // 128, 128, N // 128, 128).transpose(2, 1, 0, 3)).reshape(N // 128, 128, (K // 128) * 128)


def _tm_tiles(W):
    K, N = W.shape
    return np.ascontiguousarray(W.reshape(K // 128, 128, N // 512, 512).transpose(2, 1, 0, 3)).reshape(N // 512, 128, (K // 128) * 512)


_HPERM = np.concatenate([np.arange(0, 16), np.arange(32, 48), np.arange(16, 32), np.arange(48, 64), np.arange(64, 128)])


def prepare_shared(inp):
    f = lambda a: np.asarray(a, dtype=np.float32)
    w_in = f(inp["w_in"])
    sh = {}
    sh["wada"] = np.concatenate([_tm_tiles(f(inp["w_ada"])[l]) for l in range(L)], axis=0)
    sh["bada"] = f(inp["b_ada"])
    sh["badapp"] = np.ascontiguousarray(f(inp["b_ada"]).reshape(L, 6, 16, 128).transpose(0, 3, 1, 2)).reshape(L, 128, 96)
    tiles = []
    wvs = []
    for l in range(L):
        W = w_in[l]
        q = W[:, 0:2048].reshape(D, NH, 128)[:, :, _HPERM].reshape(D, 2048)
        k = W[:, 2048:2560].reshape(D, NKV, 128)[:, :, _HPERM].reshape(D, 512)
        rest = W[:, 3072:]
        tiles.append(_fm_tiles(np.concatenate([q, k, rest], axis=1)))
        wvs.append(_tm_tiles(W[:, 2560:3072]))
    sh["winfm"] = np.concatenate(tiles, axis=0)
    sh["wv"] = np.concatenate(wvs, axis=0)
    sh["sinkb"] = np.ascontiguousarray(np.broadcast_to(f(inp["sink"]).reshape(1, L * NH), (128, L * NH)))
    cg = f(inp["conv_ln_g"]).reshape(L, 16, 128).transpose(0, 2, 1)
    cb = f(inp["conv_ln_b"]).reshape(L, 16, 128).transpose(0, 2, 1)
    sh["clnpp"] = np.ascontiguousarray(np.concatenate([cg, cb], axis=2))
    sh["lnrows"] = np.ascontiguousarray(np.stack([f(inp["ln1_g"]), f(inp["ln1_b"]), f(inp["ln2_g"]), f(inp["ln2_b"])], axis=1))
    sh["woa"] = np.concatenate([_fm_tiles(f(inp["w_oa"])[l]) for l in range(L)], axis=0)
    sh["wob"] = np.concatenate([_fm_tiles(f(inp["w_ob"])[l]) for l in range(L)], axis=0)
    sh["wout"] = np.concatenate([_tm_tiles(f(inp["w_out"])[l]) for l in range(L)], axis=0)
    sh["wgu"] = np.concatenate([_fm_tiles(f(inp["w_gu"])[l]) for l in range(L)], axis=0)
    sh["wdown"] = np.concatenate([_tm_tiles(f(inp["w_down"])[l]) for l in range(L)], axis=0)
    sh["ident"] = np.eye(128, dtype=np.float32)
    i = np.arange(128)[:, None]
    j = np.arange(128)[None, :]
    m = np.zeros((128, 384), np.float32)
    m[:, 0:128] = np.where(j >= i, 0.0, NEG)
    m[:, 256:384] = np.where(j <= i, 0.0, NEG)
    sh["mask"] = m
    return sh


def prepare_core(inp, core):
    f = lambda a: np.asarray(a, dtype=np.float32)
    b, flip = core // 2, core % 2
    x = f(inp["x"])[b]
    S = x.shape[0]
    if flip:
        xin = np.ascontiguousarray(x[::-1][0:TMAX])
        pos = (S - 1 - np.arange(TMAX)).astype(np.float32)
    else:
        xin = np.ascontiguousarray(x[0:TMAX])
        pos = np.arange(TMAX).astype(np.float32)
    inv_freq = (np.float32(500000.0) ** (-np.arange(0, 32, 2, dtype=np.float32) / np.float32(32))).astype(np.float32)
    ang = pos[None, :] * inv_freq[:, None]
    cos, sin = np.cos(ang).astype(np.float32), np.sin(ang).astype(np.float32)
    ctab = np.ones((64, TMAX), np.float32)
    stab = np.zeros((64, TMAX), np.float32)
    ctab[0:16] = cos
    ctab[32:48] = cos
    stab[0:16] = -sin
    stab[32:48] = sin
    wd = f(inp["w_dw"])
    if flip:
        wd = wd[:, ::-1, :]
    wdw = np.ascontiguousarray(wd.reshape(L, CW, 16, 128).transpose(0, 3, 2, 1)).reshape(L, 128, 16 * CW)
    cpp = np.ascontiguousarray(f(inp["c"])[b].reshape(16, 128).T)
    return {"xin": xin, "cpp": cpp, "ctab": ctab, "stab": stab, "wdw": wdw}


_CACHE = {}


def kernel(**inputs):
    if "nc" not in _CACHE:
        _CACHE["nc"] = build_program()[0]
    nc = _CACHE["nc"]
    sh = prepare_shared(inputs)
    in_maps = []
    for core in range(8):
        m = dict(sh)
        m.update(prepare_core(inputs, core))
        in_maps.append(m)
    res = run_bass_kernel_spmd(nc, in_maps, core_ids=list(range(8)))
    x = np.asarray(inputs["x"])
    out = np.empty(x.shape, np.float32)
    S = x.shape[1]
    for core in range(8):
        b, flip = core // 2, core % 2
        yv = np.asarray(res.results[core]["y"], dtype=np.float32)
        if flip:
            out[b, S - NOWN:] = yv[::-1]
        else:
            out[b, 0:NOWN] = yv
    return out
```

```python
import numpy as np
from contextlib import ExitStack
import concourse.bass as bass
import concourse.mybir as mybir
from concourse.bass_utils import run_bass_kernel_spmd

F32 = mybir.dt.float32
BF16 = mybir.dt.bfloat16
ALU = mybir.AluOpType
AF = mybir.ActivationFunctionType
AX = mybir.AxisListType

D = 2048
KC = 16
L = 4
NH = 16
NKV = 4
DFF = 5632
FC = 44
CW = 31
TMAX = 2560
NOWN = 2048
ALPHA = (2.0 * L) ** 0.25
EPS = 1e-5
NEG = -1e30
QSCALE = 128.0 ** -0.5


def t_in(l):
    return TMAX - 128 * l


def t_out(l):
    return TMAX - 128 * (l + 1)


def blocks(T, bs=512):
    return [(t0, min(bs, T - t0)) for t0 in range(0, T, bs)]


class Eng:
    def __init__(self, name):
        self.name = name
        self.ops = []
        self.sem = None
        self.cnt = 0
        self.waited = {}
        self.pend = []


class Buf:
    __slots__ = ("w", "r", "ds", "name")

    def __init__(self, name=""):
        self.w = None
        self.r = []
        self.ds = None
        self.name = name


class DSem:
    def __init__(self, h):
        self.h = h
        self.cnt = 0


class Arena:
    def __init__(self, ap, nbytes):
        self.ap = ap
        self.n = nbytes
        self.off = 0
        self.base = 0

    def reset(self):
        self.off = self.base

    def alloc(self, nbytes, dt=BF16):
        nbytes = (nbytes + 63) // 64 * 64
        assert self.off + nbytes <= self.n, (self.off, nbytes, self.n)
        a = self.ap[:, self.off // 2:(self.off + nbytes) // 2]
        self.off += nbytes
        if dt is F32:
            a = a.bitcast(F32)
        return a


class KB:
    def __init__(self, nc, stack, nds=48):
        self.nc = nc
        self.E = {n: Eng(n) for n in ("pe", "act", "dve", "pool", "sp")}
        for n, e in self.E.items():
            e.sem = stack.enter_context(nc.semaphore("es_" + n))
        self.dsems = [DSem(stack.enter_context(nc.semaphore("ds%d" % i))) for i in range(nds)]
        self.free_ds = list(self.dsems)
        self.bar = stack.enter_context(nc.semaphore("bar"))
        self.barcnt = 0
        self.ninstr = 0

    def _wait(self, eng, toks):
        for (sid, semh, val) in toks:
            if eng.waited.get(sid, 0) >= val:
                continue
            eng.waited[sid] = val
            eng.ops.append(lambda h, semh=semh, val=val: h.wait_ge(semh, val))

    def _deps(self, reads, writes):
        toks = []
        for b in reads:
            if b.w is not None:
                toks.append(b.w)
        for b in writes:
            if b.w is not None:
                toks.append(b.w)
            toks.extend(b.r)
        return toks

    def op(self, en, fn, reads=(), writes=(), inc=True):
        eng = self.E[en]
        self._wait(eng, self._deps(reads, writes))
        self.ninstr += 1
        if inc:
            eng.cnt += 1
            tok = (id(eng), eng.sem, eng.cnt)
            eng.ops.append(lambda h, fn=fn, sem=eng.sem: fn(h).then_inc(sem, 1))
            for (b, kind) in eng.pend:
                if kind == "r":
                    b.r.append(tok)
                else:
                    b.w = tok
                    b.r = []
            eng.pend = []
            for b in reads:
                b.r.append(tok)
            for b in writes:
                b.w = tok
                b.r = []
            return tok
        eng.ops.append(fn)
        for b in reads:
            eng.pend.append((b, "r"))
        for b in writes:
            eng.pend.append((b, "w"))
        return None

    def dma(self, en, out, in_, reads=(), writes=(), key=None, **kw):
        eng = self.E[en]
        self._wait(eng, self._deps(reads, writes))
        if key.ds is None:
            key.ds = self.free_ds.pop()
        ds = key.ds
        ds.cnt += 16
        tok = (id(ds), ds.h, ds.cnt)
        self.ninstr += 1
        eng.ops.append(lambda h, out=out, in_=in_, kw=kw, dh=ds.h: h.dma_start(out=out, in_=in_, **kw).then_inc(dh, 16))
        for b in reads:
            b.r.append(tok)
        for b in writes:
            b.w = tok
            b.r = []
        return tok

    def mm(self, bank, out, pairs, reads=(), transpose=False, final=True, first=True):
        n = len(pairs)
        for i, (a, b) in enumerate(pairs):
            last = (i == n - 1) and final
            if transpose:
                fn = (lambda h, out=out, a=a, b=b: h.transpose(out=out, in_=a, identity=b))
            else:
                fn = (lambda h, out=out, a=a, b=b, s=(i == 0), e=(i == n - 1):
                      h.matmul(out, a, b, start=s, stop=e))
            self.op("pe", fn, reads=reads if i == 0 else (), writes=[bank] if (i == 0 and first) else (), inc=last)

    def barrier(self):
        sp = self.E["sp"]
        toks = []
        for n, e in self.E.items():
            if n != "sp" and e.cnt > 0:
                toks.append((id(e), e.sem, e.cnt))
        for d in self.dsems:
            if d.cnt > 0:
                toks.append((id(d), d.h, d.cnt))
        self._wait(sp, toks)
        self.barcnt += 1
        bar = self.bar
        sp.ops.append(lambda h: h.nop().then_inc(bar, 1))
        for n, e in self.E.items():
            if n != "sp":
                self._wait(e, [(id(bar), bar, self.barcnt)])
        self.free_ds = list(self.dsems)


def newbufs(n, name=""):
    return [Buf(name + str(i)) for i in range(n)]


def release(bufs):
    for b in bufs:
        b.ds = None


def build_program(nlayers=L, debug=False):
    nc = bass.Bass("TRN2", target_bir_lowering=False)
    dk = "ExternalOutput" if debug else "Internal"

    def din(name, shape, dt=F32):
        return nc.dram_tensor(name, list(shape), dt, kind="ExternalInput").ap()

    def dscr(name, shape, dt=F32):
        return nc.dram_tensor(name, list(shape), dt, kind=dk).ap()

    xin = din("xin", [TMAX, D])
    cpp_d = din("cpp", [128, 16])
    ctab_d = din("ctab", [64, TMAX])
    stab_d = din("stab", [64, TMAX])
    wdw_d = din("wdw", [L, 128, KC * CW])
    wada_d = din("wada", [L * 24, 128, KC * 512])
    bada_d = din("bada", [L, 6 * D])
    badapp_d = din("badapp", [L, 128, 96])
    winfm_d = din("winfm", [L * 84, 128, KC * 128])
    wv_d = din("wv", [L, 128, KC * 512])
    sinkb_d = din("sinkb", [128, L * NH])
    clnpp_d = din("clnpp", [L, 128, 32])
    lnrows_d = din("lnrows", [L, 4, D])
    woa_d = din("woa", [L * 16, 128, KC * 128])
    wob_d = din("wob", [L * 16, 128, KC * 128])
    wout_d = din("wout", [L * 4, 128, KC * 512])
    wgu_d = din("wgu", [L * 88, 128, KC * 128])
    wdown_d = din("wdown", [L * 4, 128, FC * 512])
    ident_d = din("ident", [128, 128])
    mask_d = din("mask", [128, 384])
    y = nc.dram_tensor("y", [NOWN, D], F32, kind="ExternalOutput").ap()

    XA = dscr("XA", [TMAX, D])
    XB = dscr("XB", [TMAX, D])
    OTd = dscr("OTd", [D, TMAX], BF16)
    Cd = dscr("Cd", [D, TMAX], BF16)
    SGd = dscr("SGd", [2 * D, TMAX], BF16)
    MTd = dscr("MTd", [D, TMAX], BF16)
    Zd = dscr("Zd", [TMAX, D])
    Hd = dscr("Hd", [DFF, TMAX], BF16)
    Gd = dscr("Gd", [2 * L, 128, D])

    RA_BYTES = 81920
    RB_BYTES = 90112
    MS_BYTES = 38400

    with ExitStack() as stack:
        ec = stack.enter_context
        RAt = ec(nc.sbuf_tensor("RA", [128, RA_BYTES // 2], BF16))
        RBt = ec(nc.sbuf_tensor("RB", [128, RB_BYTES // 2], BF16))
        MSt = ec(nc.sbuf_tensor("MS", [128, MS_BYTES // 2], BF16))
        pst = [ec(nc.psum_tensor("ps%d" % i, [128, 512], F32)) for i in range(8)]
        kb = KB(nc, stack)
        RA = Arena(RAt, RA_BYTES)
        RB = Arena(RBt, RB_BYTES)
        MS = Arena(MSt, MS_BYTES)
        PS = [p[:, :] for p in pst]
        PB = newbufs(8, "bank")

        ident_f = MS.alloc(512, F32)
        ident_b = MS.alloc(256)
        onesm = MS.alloc(256)
        mask = MS.alloc(384 * 4, F32)
        sinkb = MS.alloc(L * NH * 4, F32)
        modpp = MS.alloc(L * 64 * 4, F32)
        cactf = MS.alloc(64, F32)
        cact = MS.alloc(64)
        wdw = MS.alloc(KC * CW * 4, F32)
        clnpp = MS.alloc(32 * 4, F32)
        small = MS.alloc(1024, F32)
        MS.base = MS.off

        cb_ = Buf("const")
        kb.dma("sp", ident_f, ident_d, writes=[cb_], key=cb_)
        cb2 = Buf("const2")
        kb.dma("sp", mask, mask_d, writes=[cb2], key=cb2)
        cb3 = Buf("const3")
        kb.dma("sp", sinkb, sinkb_d, writes=[cb3], key=cb3)
        cb4 = Buf("const4")
        kb.dma("sp", cactf[:, 0:16], cpp_d, writes=[cb4], key=cb4)
        kb.op("dve", lambda h: h.tensor_copy(out=ident_b, in_=ident_f), reads=[cb_])
        kb.op("dve", lambda h: h.memset(onesm, 1.0 / D))
        kb.op("act", lambda h: h.activation(out=cactf[:, 0:16], in_=cactf[:, 0:16], func=AF.Silu), reads=[cb4], writes=[cb4])
        kb.op("act", lambda h: h.activation(out=cact[:, 0:16], in_=cactf[:, 0:16], func=AF.Copy), reads=[cb4])
        kb.barrier()

        def hview(ar_ap, T):
            return ar_ap[:, 0:KC * T].rearrange("p (k t) -> p k t", t=T)

        def wdma(slot3, src2, buf):
            flat = slot3.rearrange("p k t -> p (k t)")
            n = flat.shape[1]
            if n > 2048:
                flat = flat.rearrange("p (a b) -> p a b", b=2048)
                src2 = src2.rearrange("p (a b) -> p a b", b=2048)
            kb.dma("pool", flat, src2, writes=[buf], key=buf)

        def phase_mod():
            RB.reset()
            MS.reset()
            crep = RB.alloc(KC * 128 * 2).rearrange("p (k t) -> p k t", t=128)
            zt = RB.alloc(128 * 4, F32)
            kb.op("dve", lambda h: h.memset(zt, 0.0))
            for kc in range(KC):
                kb.op("dve", lambda h, kc=kc: h.tensor_scalar(out=crep[:, kc, :], in0=zt, scalar1=cactf[:, kc:kc + 1],
                                                              scalar2=None, op0=ALU.add))
            wsl = [RB.alloc(KC * 512 * 2).rearrange("p (k t) -> p k t", t=512) for _ in range(2)]
            wb = newbufs(2, "mw")
            brow = [RB.alloc(512 * 4, F32) for _ in range(2)]
            bb = newbufs(2, "brow")
            gst = [RB.alloc(512 * 4, F32) for _ in range(2)]
            gb = newbufs(2, "gst")
            bpp = RB.alloc(96 * 4, F32)
            bpb = Buf("bpp")
            kb.barrier()
            it = 0
            ig = 0
            for l in range(nlayers):
                kb.dma("sp", bpp[:, 0:96], badapp_d[l], writes=[bpb], key=bpb)
                for ti in range(24):
                    v, cbk = ti // 4, ti % 4
                    s = it % 2
                    it += 1
                    wdma(wsl[s], wada_d[l * 24 + ti], wb[s])
                    bank = it % 2
                    if v in (2, 5):
                        g = ig % 2
                        ig += 1
                        kb.dma("sp", brow[g], bada_d[l, ti * 512:(ti + 1) * 512].partition_broadcast(128), writes=[bb[g]], key=bb[g])
                        kb.mm(PB[bank], PS[bank], [(crep[:, kc, :], wsl[s][:, kc, :]) for kc in range(KC)], reads=[wb[s]])
                        kb.op("dve", lambda h, g=g, bank=bank: h.scalar_tensor_tensor(out=gst[g], in0=PS[bank], scalar=1.0, in1=brow[g],
                                                                                   op0=ALU.add, op1=ALU.add),
                              reads=[PB[bank], bb[g]], writes=[gb[g]])
                        kb.dma("sp", Gd[l * 2 + (1 if v == 5 else 0), :, cbk * 512:(cbk + 1) * 512], gst[g], reads=[gb[g]], key=gb[g])
                    else:
                        vi = {0: 0, 1: 1, 3: 2, 4: 3}[v]
                        for j in range(4):
                            kb.mm(PB[bank], PS[bank][:, j:j + 1],
                                  [(wsl[s][:, kc, j * 128:(j + 1) * 128], cact[:, kc:kc + 1]) for kc in range(KC)],
                                  reads=[wb[s]], final=(j == 3), first=(j == 0))
                        col = l * 64 + vi * 16 + cbk * 4
                        kb.op("dve", lambda h, bank=bank, col=col, v=v, cbk=cbk: h.scalar_tensor_tensor(
                            out=modpp[:, col:col + 4], in0=PS[bank][:, 0:4], scalar=(1.0 if v in (1, 4) else 0.0),
                            in1=bpp[:, v * 16 + cbk * 4:v * 16 + cbk * 4 + 4], op0=ALU.add, op1=ALU.add),
                            reads=[PB[bank], bpb])
            kb.barrier()

        def emit_hT(src, srcbuf, i, hT, mod_sc, mod_sh):
            for q in range(4):
                bank = 4 + q
                for j in range(4):
                    kc = q * 4 + j
                    kb.mm(PB[bank], PS[bank][:, j * 128:(j + 1) * 128], [(src[:, kc * 128:(kc + 1) * 128], ident_f)],
                          reads=[srcbuf], transpose=True, final=(j == 3), first=(j == 0))
                for j in range(4):
                    kc = q * 4 + j
                    kb.op("act", lambda h, bank=bank, j=j, kc=kc: h.activation(
                        out=hT[:, kc, i * 128:(i + 1) * 128], in_=PS[bank][:, j * 128:(j + 1) * 128], func=AF.Identity,
                        scale=modpp[:, mod_sc + kc:mod_sc + kc + 1], bias=modpp[:, mod_sh + kc:mod_sh + kc + 1]),
                        reads=[PB[bank]])

        def phase_prep(l):
            T = t_in(l)
            RA.reset()
            RB.reset()
            hT = hview(RA.alloc(KC * T * 2), T)
            xs = [RB.alloc(D * 4, F32) for _ in range(3)]
            xb = newbufs(3, "xs")
            nt = T // 128
            for i in range(nt):
                s = i % 3
                kb.dma("sp", xs[s], xin[i * 128:(i + 1) * 128, :], writes=[xb[s]], key=xb[s])
                emit_hT(xs[s], xb[s], i, hT, l * 64 + 16, l * 64 + 0)
            kb.barrier()
            return hT

        def phase_v(l, hT):
            T = t_in(l)
            RB.reset()
            v_all = RB.alloc((TMAX // 128) * 512 * 2).rearrange("p (i d) -> p i d", d=512)
            wt = RB.alloc(KC * 512 * 2).rearrange("p (k t) -> p k t", t=512)
            wbuf = Buf("wv")
            vbase = RB.off
            wdma(wt, wv_d[l], wbuf)
            for i in range(T // 128):
                bank = i % 4
                kb.mm(PB[bank], PS[bank], [(hT[:, kc, i * 128:(i + 1) * 128], wt[:, kc, :]) for kc in range(KC)], reads=[wbuf])
                kb.op("act", lambda h, i=i, bank=bank: h.activation(out=v_all[:, i, :], in_=PS[bank], func=AF.Copy), reads=[PB[bank]])
            kb.barrier()
            return v_all, vbase - 16384

        def phase_qk_attn(l, hT, v_all, rb_base):
            Ti, To = t_in(l), t_out(l)
            nq = To // 128
            RB.off = rb_base
            qT = RB.alloc(4 * TMAX * 2).rearrange("p (j t) -> p j t", t=TMAX)
            kT = RB.alloc(TMAX * 2)
            ctab = RB.alloc(TMAX * 4, F32)
            stab = RB.alloc(TMAX * 4, F32)
            tb_ = Buf("tabs")
            kb.dma("sp", ctab[0:64, :], ctab_d, writes=[tb_], key=tb_)
            tb2 = Buf("tabs2")
            kb.dma("sp", stab[0:64, :], stab_d, writes=[tb2], key=tb2)
            base2 = RB.off
            for g in range(NKV):
                RB.off = base2
                MS.reset()
                wsl = [MS.alloc(KC * 128 * 2).rearrange("p (k t) -> p k t", t=128) for _ in range(3)]
                wb = newbufs(3, "wqk")
                qf = [RB.alloc(512 * 4, F32) for _ in range(2)]
                qfb = newbufs(2, "qf")
                tmp = [RB.alloc(512 * 4, F32) for _ in range(2)]
                tmb = newbufs(2, "tmp")
                r1 = [RB.alloc(512 * 4, F32) for _ in range(2)]
                r1b = newbufs(2, "r1")
                r2 = [RB.alloc(512 * 4, F32) for _ in range(2)]
                r2b = newbufs(2, "r2")
                it = 0
                for ci in range(5):
                    isk = ci == 4
                    tile = winfm_d[l * 84 + (16 + g if isk else 4 * g + ci)]
                    s = ci % 3
                    wdma(wsl[s], tile, wb[s])
                    dest = kT if isk else qT[:, ci, :]
                    sc = 1.0 if isk else QSCALE
                    for (t0, n) in blocks(Ti if isk else To):
                        bank = it % 4
                        u = it % 2
                        it += 1
                        kb.mm(PB[bank], PS[bank][:, 0:n], [(wsl[s][:, kc, :], hT[:, kc, t0:t0 + n]) for kc in range(KC)], reads=[wb[s]])
                        kb.op("act", lambda h, bank=bank, u=u, n=n, sc=sc: h.activation(out=qf[u][0:64, 0:n], in_=PS[bank][0:64, 0:n],
                                                                                      func=AF.Copy, scale=sc),
                              reads=[PB[bank]], writes=[qfb[u]])
                        kb.op("act", lambda h, bank=bank, n=n, sc=sc, dest=dest, t0=t0: h.activation(
                            out=dest[64:128, t0:t0 + n], in_=PS[bank][64:128, 0:n], func=AF.Copy, scale=sc), reads=[PB[bank]])
                        kb.op("dve", lambda h, u=u, n=n: h.tensor_copy(out=tmp[u][0:32, 0:n], in_=qf[u][32:64, 0:n]),
                              reads=[qfb[u]], writes=[tmb[u]])
                        kb.op("dve", lambda h, u=u, n=n: h.tensor_copy(out=tmp[u][32:64, 0:n], in_=qf[u][0:32, 0:n]),
                              reads=[qfb[u]], writes=[tmb[u]])
                        kb.op("dve", lambda h, u=u, n=n, t0=t0: h.tensor_tensor(out=r1[u][0:64, 0:n], in0=qf[u][0:64, 0:n],
                                                                              in1=ctab[0:64, t0:t0 + n], op=ALU.mult),
                              reads=[qfb[u], tb_], writes=[r1b[u]])
                        kb.op("dve", lambda h, u=u, n=n, t0=t0: h.tensor_tensor(out=r2[u][0:64, 0:n], in0=tmp[u][0:64, 0:n],
                                                                              in1=stab[0:64, t0:t0 + n], op=ALU.mult),
                              reads=[tmb[u], tb2], writes=[r2b[u]])
                        kb.op("dve", lambda h, u=u, n=n, t0=t0, dest=dest: h.tensor_tensor(out=dest[0:64, t0:t0 + n], in0=r1[u][0:64, 0:n],
                                                                                         in1=r2[u][0:64, 0:n], op=ALU.add),
                              reads=[r1b[u], r2b[u]])
                kb.barrier()
                RB.off = base2
                sm = [RB.alloc(384 * 4, F32) for _ in range(2)]
                smb = newbufs(2, "sm")
                Pm = [RB.alloc(384 * 2) for _ in range(2)]
                Pb_ = newbufs(2, "P")
                Pn = [RB.alloc(384 * 2) for _ in range(2)]
                Pnb = newbufs(2, "Pn")
                PTs = [RB.alloc(384 * 2) for _ in range(2)]
                PTb = newbufs(2, "PTs")
                st = [RB.alloc(64, F32) for _ in range(2)]
                stb = newbufs(2, "st")
                OTst = [RB.alloc(4 * 128 * 2).rearrange("p (j t) -> p j t", t=128) for _ in range(2)]
                OTb = newbufs(2, "OTst")
                OTv = OTd.rearrange("(h p) t -> p h t", p=128)
                items = [(n, j) for n in range(nq) for j in range(4)]
                NI = len(items)

                def stA(s):
                    n, j = items[s]
                    u = s % 2
                    hcol = l * NH + 4 * g + j
                    nk = 2 if n == 0 else 3
                    ks = max(n - 1, 0) * 128
                    nkeys = nk * 128
                    mo = 128 if n == 0 else 0
                    bank = u
                    kb.mm(PB[bank], PS[bank][:, 0:nkeys], [(qT[:, j, n * 128:(n + 1) * 128], kT[:, ks:ks + nkeys])])
                    kb.op("dve", lambda h: h.tensor_tensor(out=sm[u][:, 0:nkeys], in0=PS[bank][:, 0:nkeys], in1=mask[:, mo:mo + nkeys], op=ALU.add),
                          reads=[PB[bank]], writes=[smb[u]])
                    kb.op("dve", lambda h: h.reduce_max(out=st[u][:, 0:1], in_=sm[u][:, 0:nkeys], axis=AX.X), reads=[smb[u]], writes=[stb[u]])
                    kb.op("dve", lambda h: h.tensor_scalar(out=st[u][:, 1:2], in0=st[u][:, 0:1], scalar1=sinkb[:, hcol:hcol + 1], scalar2=-1.0,
                                                           op0=ALU.max, op1=ALU.mult), reads=[stb[u]], writes=[stb[u]])
                    kb.op("act", lambda h: h.activation(out=Pm[u][:, 0:nkeys], in_=sm[u][:, 0:nkeys], func=AF.Exp, bias=st[u][:, 1:2], scale=1.0,
                                                        accum_out=st[u][:, 2:3]), reads=[smb[u], stb[u]], writes=[Pb_[u], stb[u]])
                    kb.op("act", lambda h: h.activation(out=st[u][:, 3:4], in_=sinkb[:, hcol:hcol + 1], func=AF.Exp, bias=st[u][:, 1:2], scale=1.0),
                          reads=[stb[u]], writes=[stb[u]])
                    kb.op("dve", lambda h: h.tensor_tensor(out=st[u][:, 4:5], in0=st[u][:, 2:3], in1=st[u][:, 3:4], op=ALU.add),
                          reads=[stb[u]], writes=[stb[u]])
                    kb.op("dve", lambda h: h.reciprocal(out=st[u][:, 5:6], in_=st[u][:, 4:5]), reads=[stb[u]], writes=[stb[u]])
                    kb.op("dve", lambda h: h.tensor_scalar(out=Pn[u][:, 0:nkeys], in0=Pm[u][:, 0:nkeys], scalar1=st[u][:, 5:6], scalar2=None,
                                                           op0=ALU.mult), reads=[Pb_[u], stb[u]], writes=[Pnb[u]])

                def stB(s):
                    n, j = items[s]
                    u = s % 2
                    nk = 2 if n == 0 else 3
                    bank = 2 + u
                    pv = PS[bank].bitcast(BF16)
                    for kbk in range(nk):
                        kb.mm(PB[bank], pv[:, kbk * 128:(kbk + 1) * 128], [(Pn[u][:, kbk * 128:(kbk + 1) * 128], ident_b)],
                              reads=[Pnb[u]], transpose=True, final=(kbk == nk - 1), first=(kbk == 0))
                    kb.op("act", lambda h: h.activation(out=PTs[u][:, 0:nk * 128], in_=pv[:, 0:nk * 128], func=AF.Copy),
                          reads=[PB[bank]], writes=[PTb[u]])

                def stC(s):
                    n, j = items[s]
                    u = s % 2
                    nk = 2 if n == 0 else 3
                    kb0 = max(n - 1, 0)
                    bank = 4 + u
                    o = n % 2
                    kb.mm(PB[bank], PS[bank][:, 0:128],
                          [(v_all[:, kb0 + kbk, g * 128:(g + 1) * 128], PTs[u][:, kbk * 128:(kbk + 1) * 128]) for kbk in range(nk)],
                          reads=[PTb[u]])
                    kb.op("act", lambda h: h.activation(out=OTst[o][:, j, :], in_=PS[bank][:, 0:128], func=AF.Copy),
                          reads=[PB[bank]], writes=[OTb[o]])
                    if j == 3:
                        kb.dma("sp", OTv[:, 4 * g:4 * g + 4, n * 128:(n + 1) * 128], OTst[o], reads=[OTb[o]], key=OTb[o])

                for s in range(NI + 2):
                    if s < NI:
                        stA(s)
                    if 0 <= s - 1 < NI:
                        stB(s - 1)
                    if 0 <= s - 2 < NI:
                        stC(s - 2)
                kb.barrier()

        def phase_conv_gates(l, hT):
            Ti, To = t_in(l), t_out(l)
            RB.reset()
            MS.reset()
            wsl = [RB.alloc(KC * 128 * 2).rearrange("p (k t) -> p k t", t=128) for _ in range(8)]
            wb = newbufs(8, "wcg")
            ub = [RB.alloc((15 + TMAX) * 4, F32) for _ in range(2)]
            ubb = newbufs(2, "u")
            acc = RB.alloc(TMAX * 4, F32)
            accb = Buf("acc")
            Cst = [RB.alloc(TMAX * 2) for _ in range(2)]
            Cstb = newbufs(2, "Cst")
            sg = [RB.alloc(512 * 4, F32) for _ in range(2)]
            sgb = newbufs(2, "sg")
            gs = [RB.alloc(512 * 2) for _ in range(4)]
            gsb = newbufs(4, "gs")
            wdb = Buf("wdw")
            kb.dma("sp", wdw, wdw_d[l], writes=[wdb], key=wdb)
            for u in range(2):
                kb.op("dve", lambda h, u=u: h.memset(ub[u][:, 0:15], 0.0), writes=[ubb[u]])
            ia = 0
            ig = 0
            for c in range(KC):
                base = l * 84
                tiles = [base + 20 + c, base + 36 + c, base + 52 + c, base + 68 + c]
                ws = [(c % 2) * 4 + q for q in range(4)]
                for q in range(4):
                    wdma(wsl[ws[q]], winfm_d[tiles[q]], wb[ws[q]])
                uu = c % 2
                for (t0, n) in blocks(Ti):
                    ba, bb = (ia % 2) * 2, (ia % 2) * 2 + 1
                    s = ia % 2
                    ia += 1
                    kb.mm(PB[ba], PS[ba][:, 0:n], [(wsl[ws[0]][:, kc, :], hT[:, kc, t0:t0 + n]) for kc in range(KC)], reads=[wb[ws[0]]])
                    kb.mm(PB[bb], PS[bb][:, 0:n], [(wsl[ws[1]][:, kc, :], hT[:, kc, t0:t0 + n]) for kc in range(KC)], reads=[wb[ws[1]]])
                    kb.op("act", lambda h, bb=bb, s=s, n=n: h.activation(out=sg[s][:, 0:n], in_=PS[bb][:, 0:n], func=AF.Sigmoid),
                          reads=[PB[bb]], writes=[sgb[s]])
                    kb.op("dve", lambda h, ba=ba, s=s, n=n, t0=t0, uu=uu: h.tensor_tensor(out=ub[uu][:, 15 + t0:15 + t0 + n], in0=PS[ba][:, 0:n],
                                                                                        in1=sg[s][:, 0:n], op=ALU.mult),
                          reads=[PB[ba], sgb[s]], writes=[ubb[uu]])
                for which in range(2):
                    for (t0, n) in blocks(To):
                        bank = 4 + ig % 4
                        s = ig % 4
                        ig += 1
                        kb.mm(PB[bank], PS[bank][:, 0:n], [(wsl[ws[2 + which]][:, kc, :], hT[:, kc, t0:t0 + n]) for kc in range(KC)],
                              reads=[wb[ws[2 + which]]])
                        kb.op("act", lambda h, bank=bank, s=s, n=n: h.activation(out=gs[s][:, 0:n], in_=PS[bank][:, 0:n], func=AF.Sigmoid),
                              reads=[PB[bank]], writes=[gsb[s]])
                        kb.dma("sp", SGd[which * D + c * 128:which * D + (c + 1) * 128, t0:t0 + n], gs[s][:, 0:n], reads=[gsb[s]], key=gsb[s])
                cs = c % 2
                for k in range(CW):
                    wk = wdw[:, c * CW + k:c * CW + k + 1]
                    if k == 0:
                        kb.op("dve", lambda h, uu=uu, wk=wk: h.tensor_scalar(out=acc[:, 0:To], in0=ub[uu][:, 0:To], scalar1=wk, scalar2=None, op0=ALU.mult),
                              reads=[ubb[uu], wdb], writes=[accb])
                    elif k < CW - 1:
                        kb.op("dve", lambda h, uu=uu, wk=wk, k=k: h.scalar_tensor_tensor(out=acc[:, 0:To], in0=ub[uu][:, k:k + To], scalar=wk,
                                                                                         in1=acc[:, 0:To], op0=ALU.mult, op1=ALU.add),
                              reads=[ubb[uu], accb], writes=[accb])
                    else:
                        kb.op("dve", lambda h, uu=uu, wk=wk, k=k, cs=cs: h.scalar_tensor_tensor(out=Cst[cs][:, 0:To], in0=ub[uu][:, k:k + To], scalar=wk,
                                                                                                in1=acc[:, 0:To], op0=ALU.mult, op1=ALU.add),
                              reads=[ubb[uu], accb], writes=[Cstb[cs]])
                kb.dma("sp", Cd[c * 128:(c + 1) * 128, 0:To], Cst[cs][:, 0:To], reads=[Cstb[cs]], key=Cstb[cs])
            kb.barrier()

        def phase_convln(l):
            To = t_out(l)
            RA.reset()
            RB.reset()
            cbT = hview(RA.alloc(KC * To * 2), To)
            Cb = [RB.alloc(KC * 512 * 2).rearrange("p (k t) -> p k t", t=512) for _ in range(2)]
            Cbb = newbufs(2, "Cb")
            SQ = RB.alloc(KC * 512 * 2).rearrange("p (k t) -> p k t", t=512)
            SQb = Buf("SQ")
            msb = RB.alloc(512 * 4, F32)
            m2 = RB.alloc(512 * 4, F32)
            rstd = RB.alloc(512 * 4, F32)
            stb = Buf("lnst")
            t1 = [RB.alloc(512 * 4, F32) for _ in range(2)]
            t1b = newbufs(2, "t1")
            t2 = [RB.alloc(512 * 4, F32) for _ in range(2)]
            t2b = newbufs(2, "t2")
            pb = Buf("clnpp")
            kb.dma("sp", clnpp, clnpp_d[l], writes=[pb], key=pb)
            Cv = Cd.rearrange("(j p) t -> p j t", p=128)
            bl = blocks(To)
            for bi in range(len(bl) + 1):
                if bi < len(bl):
                    t0, n = bl[bi]
                    kb.dma("sp", Cb[bi % 2][:, :, 0:n], Cv[:, :, t0:t0 + n], writes=[Cbb[bi % 2]], key=Cbb[bi % 2])
                if bi == 0:
                    continue
                t0, n = bl[bi - 1]
                s = (bi - 1) % 2
                kb.op("act", lambda h, s=s, n=n: h.activation(out=SQ[:, :, 0:n], in_=Cb[s][:, :, 0:n], func=AF.Square),
                      reads=[Cbb[s]], writes=[SQb])
                kb.mm(PB[0], PS[0][:, 0:n], [(onesm, Cb[s][:, j, 0:n]) for j in range(KC)], reads=[Cbb[s]])
                kb.mm(PB[1], PS[1][:, 0:n], [(onesm, SQ[:, j, 0:n]) for j in range(KC)], reads=[SQb])
                kb.op("act", lambda h, n=n: h.activation(out=msb[:, 0:n], in_=PS[0][:, 0:n], func=AF.Copy), reads=[PB[0]], writes=[stb])
                kb.op("dve", lambda h, n=n: h.tensor_tensor(out=m2[:, 0:n], in0=msb[:, 0:n], in1=msb[:, 0:n], op=ALU.mult), reads=[stb], writes=[stb])
                kb.op("dve", lambda h, n=n: h.tensor_tensor(out=m2[:, 0:n], in0=PS[1][:, 0:n], in1=m2[:, 0:n], op=ALU.subtract),
                      reads=[PB[1], stb], writes=[stb])
                kb.op("dve", lambda h, n=n: h.tensor_scalar(out=m2[:, 0:n], in0=m2[:, 0:n], scalar1=0.0, scalar2=EPS, op0=ALU.max, op1=ALU.add),
                      reads=[stb], writes=[stb])
                kb.op("act", lambda h, n=n: h.activation(out=m2[:, 0:n], in_=m2[:, 0:n], func=AF.Sqrt), reads=[stb], writes=[stb])
                kb.op("dve", lambda h, n=n: h.reciprocal(out=rstd[:, 0:n], in_=m2[:, 0:n]), reads=[stb], writes=[stb])
                for j in range(KC):
                    q = j % 2
                    kb.op("dve", lambda h, s=s, j=j, q=q, n=n: h.tensor_tensor(out=t1[q][:, 0:n], in0=Cb[s][:, j, 0:n], in1=msb[:, 0:n], op=ALU.subtract),
                          reads=[Cbb[s], stb], writes=[t1b[q]])
                    kb.op("dve", lambda h, q=q, n=n: h.tensor_tensor(out=t2[q][:, 0:n], in0=t1[q][:, 0:n], in1=rstd[:, 0:n], op=ALU.mult),
                          reads=[t1b[q], stb], writes=[t2b[q]])
                    kb.op("act", lambda h, j=j, q=q, n=n, t0=t0: h.activation(out=cbT[:, j, t0:t0 + n], in_=t2[q][:, 0:n], func=AF.Silu,
                                                                            scale=clnpp[:, j:j + 1], bias=clnpp[:, 16 + j:17 + j]),
                          reads=[t2b[q], pb])
            kb.barrier()
            return cbT

        def phase_merge(l, cbT):
            To = t_out(l)
            RB.reset()
            MS.reset()
            OT = hview(RB.alloc(KC * To * 2), To)
            otb = newbufs(4, "OTld")
            OTv = OTd.rearrange("(h p) t -> p h t", p=128)
            for q in range(4):
                kb.dma("sp", OT[:, 4 * q:4 * q + 4, :], OTv[:, 4 * q:4 * q + 4, 0:To], writes=[otb[q]], key=otb[q])
            sga = [RB.alloc(512 * 2) for _ in range(2)]
            sgab = newbufs(2, "sga")
            sgb_ = [RB.alloc(512 * 2) for _ in range(2)]
            sgbb = newbufs(2, "sgb")
            tA = [MS.alloc(512 * 4, F32) for _ in range(2)]
            tAb = newbufs(2, "tA")
            tB = [MS.alloc(512 * 4, F32) for _ in range(2)]
            tBb = newbufs(2, "tB")
            mst = [MS.alloc(512 * 2) for _ in range(2)]
            mstb = newbufs(2, "mst")
            wsl = [MS.alloc(KC * 128 * 2).rearrange("p (k t) -> p k t", t=128) for _ in range(4)]
            wb = newbufs(4, "wmg")
            items = [(c, t0, n) for c in range(KC) for (t0, n) in blocks(To)]
            NI = len(items)

            def loads(s):
                c, t0, n = items[s]
                u = s % 2
                if t0 == 0:
                    w0, w1 = (c % 2) * 2, (c % 2) * 2 + 1
                    wdma(wsl[w0], woa_d[l * 16 + c], wb[w0])
                    wdma(wsl[w1], wob_d[l * 16 + c], wb[w1])
                kb.dma("sp", sga[u][:, 0:n], SGd[c * 128:(c + 1) * 128, t0:t0 + n], writes=[sgab[u]], key=sgab[u])
                kb.dma("sp", sgb_[u][:, 0:n], SGd[D + c * 128:D + (c + 1) * 128, t0:t0 + n], writes=[sgbb[u]], key=sgbb[u])

            def comp(s):
                c, t0, n = items[s]
                u = s % 2
                w0, w1 = (c % 2) * 2, (c % 2) * 2 + 1
                ba, bb = u * 2, u * 2 + 1
                kb.mm(PB[ba], PS[ba][:, 0:n], [(wsl[w0][:, kc, :], OT[:, kc, t0:t0 + n]) for kc in range(KC)], reads=[wb[w0]] + otb)
                kb.mm(PB[bb], PS[bb][:, 0:n], [(wsl[w1][:, kc, :], cbT[:, kc, t0:t0 + n]) for kc in range(KC)], reads=[wb[w1]])
                kb.op("dve", lambda h: h.tensor_tensor(out=tA[u][:, 0:n], in0=PS[ba][:, 0:n], in1=sga[u][:, 0:n], op=ALU.mult),
                      reads=[PB[ba], sgab[u]], writes=[tAb[u]])
                kb.op("dve", lambda h: h.tensor_tensor(out=tB[u][:, 0:n], in0=PS[bb][:, 0:n], in1=sgb_[u][:, 0:n], op=ALU.mult),
                      reads=[PB[bb], sgbb[u]], writes=[tBb[u]])
                kb.op("dve", lambda h: h.tensor_tensor(out=mst[u][:, 0:n], in0=tA[u][:, 0:n], in1=tB[u][:, 0:n], op=ALU.add),
                      reads=[tAb[u], tBb[u]], writes=[mstb[u]])
                kb.dma("sp", MTd[c * 128:(c + 1) * 128, t0:t0 + n], mst[u][:, 0:n], reads=[mstb[u]], key=mstb[u])

            for s in range(NI + 1):
                if s < NI:
                    loads(s)
                if s >= 1:
                    comp(s - 1)
            kb.barrier()

        def gemm_tm(l, To, A_view, Abufs_for_tile, wtile_fn, nkc, wsl, wb, zst, zstb, tiles):
            for nb in range(4):
                s = nb % 2
                wdma(wsl[s], wtile_fn(nb), wb[s])
                for (i, li) in tiles:
                    gemm_tm.cnt += 1
                    bank = gemm_tm.cnt % 4
                    z = gemm_tm.cnt % 3
                    kb.mm(PB[bank], PS[bank], [(A_view[:, kc, li * 128:(li + 1) * 128], wsl[s][:, kc, :]) for kc in range(nkc)],
                          reads=[wb[s]] + Abufs_for_tile)
                    kb.op("act", lambda h, bank=bank, z=z: h.activation(out=zst[z], in_=PS[bank], func=AF.Copy), reads=[PB[bank]], writes=[zstb[z]])
                    kb.dma("sp", Zd[i * 128:(i + 1) * 128, nb * 512:(nb + 1) * 512], zst[z], reads=[zstb[z]], key=zstb[z])
        gemm_tm.cnt = 0

        def phase_wout(l):
            To = t_out(l)
            RA.reset()
            RB.reset()
            MS.reset()
            MT = hview(RA.alloc(KC * To * 2), To)
            mtb = newbufs(4, "MTld")
            MTv = MTd.rearrange("(c p) t -> p c t", p=128)
            for q in range(4):
                kb.dma("sp", MT[:, 4 * q:4 * q + 4, :], MTv[:, 4 * q:4 * q + 4, 0:To], writes=[mtb[q]], key=mtb[q])
            wsl = [RB.alloc(KC * 512 * 2).rearrange("p (k t) -> p k t", t=512) for _ in range(2)]
            wb = newbufs(2, "wout")
            zst = [MS.alloc(512 * 4, F32) for _ in range(3)]
            zstb = newbufs(3, "zst")
            gemm_tm(l, To, MT, mtb, lambda nb: wout_d[l * 4 + nb], KC, wsl, wb, zst, zstb,
                    [(i, i) for i in range(To // 128)])
            kb.barrier()

        def phase_ln(l, which, xsrc, xdst, hmods, final=False):
            To = t_out(l)
            nt = (NOWN // 128) if final else (To // 128)
            RA.reset()
            RB.reset()
            MS.reset()
            hT = None if final else hview(RA.alloc(KC * To * 2), To)
            zs = [RB.alloc(D * 4, F32) for _ in range(2)]
            zb = newbufs(2, "zs")
            xs = [RB.alloc(D * 4, F32) for _ in range(2)]
            xb = newbufs(2, "xs")
            G = RB.alloc(D * 4, F32)
            lng = RB.alloc(D * 4, F32)
            lnb = RB.alloc(D * 4, F32)
            tA = RB.alloc(D * 4, F32)
            tAb = Buf("tA")
            tB = RB.alloc(D * 4, F32)
            tBb = Buf("tB")
            xo = [RB.alloc(D * 4, F32) for _ in range(2)]
            xob = newbufs(2, "xo")
            stt = MS.alloc(32 * 4, F32)
            sttb = Buf("lnstat")
            gb_, lgb, lbb = Buf("G"), Buf("lng"), Buf("lnb")
            kb.dma("sp", G, Gd[l * 2 + which], writes=[gb_], key=gb_)
            kb.dma("sp", lng, lnrows_d[l, 2 * which, :].partition_broadcast(128), writes=[lgb], key=lgb)
            kb.dma("sp", lnb, lnrows_d[l, 2 * which + 1, :].partition_broadcast(128), writes=[lbb], key=lbb)

            def loads(i):
                s = i % 2
                kb.dma("sp", zs[s], Zd[i * 128:(i + 1) * 128, :], writes=[zb[s]], key=zb[s])
                kb.dma("sp", xs[s], xsrc[i * 128:(i + 1) * 128, :], writes=[xb[s]], key=xb[s])

            def comp(i):
                s = i % 2
                kb.op("dve", lambda h: h.tensor_tensor(out=tA, in0=zs[s], in1=G, op=ALU.mult), reads=[zb[s], gb_], writes=[tAb])
                kb.op("dve", lambda h: h.scalar_tensor_tensor(out=tB, in0=xs[s], scalar=ALPHA, in1=tA, op0=ALU.mult, op1=ALU.add),
                      reads=[xb[s], tAb], writes=[tBb])
                for q in range(4):
                    kb.op("dve", lambda h, q=q: h.bn_stats(out=stt[:, q * 6:(q + 1) * 6], in_=tB[:, q * 512:(q + 1) * 512]),
                          reads=[tBb], writes=[sttb])
                kb.op("dve", lambda h: h.bn_aggr(out=stt[:, 24:26], in_=stt[:, 0:24]), reads=[sttb], writes=[sttb])
                kb.op("dve", lambda h: h.tensor_scalar(out=stt[:, 26:27], in0=stt[:, 25:26], scalar1=0.0, scalar2=EPS, op0=ALU.max, op1=ALU.add),
                      reads=[sttb], writes=[sttb])
                kb.op("act", lambda h: h.activation(out=stt[:, 27:28], in_=stt[:, 26:27], func=AF.Sqrt), reads=[sttb], writes=[sttb])
                kb.op("dve", lambda h: h.reciprocal(out=stt[:, 28:29], in_=stt[:, 27:28]), reads=[sttb], writes=[sttb])
                kb.op("dve", lambda h: h.tensor_scalar(out=tA, in0=tB, scalar1=stt[:, 24:25], scalar2=stt[:, 28:29], op0=ALU.subtract, op1=ALU.mult),
                      reads=[tBb, sttb], writes=[tAb])
                kb.op("dve", lambda h: h.tensor_tensor(out=tB, in0=tA, in1=lng, op=ALU.mult), reads=[tAb, lgb], writes=[tBb])
                kb.op("dve", lambda h: h.tensor_tensor(out=xo[s], in0=tB, in1=lnb, op=ALU.add), reads=[tBb, lbb], writes=[xob[s]])
                kb.dma("sp", xdst[i * 128:(i + 1) * 128, :], xo[s], reads=[xob[s]], key=xob[s])
                if not final:
                    emit_hT(xo[s], xob[s], i, hT, hmods[0], hmods[1])

            for i in range(nt + 1):
                if i < nt:
                    loads(i)
                if i >= 1:
                    comp(i - 1)
            kb.barrier()
            return hT

        def phase_ffn_up(l, hT2):
            To = t_out(l)
            RB.reset()
            MS.reset()
            wsl = [MS.alloc(KC * 128 * 2).rearrange("p (k t) -> p k t", t=128) for _ in range(4)]
            wb = newbufs(4, "wgu")
            sg = [RB.alloc(512 * 4, F32) for _ in range(2)]
            sgb = newbufs(2, "sgf")
            hst = [RB.alloc(TMAX * 2) for _ in range(2)]
            hstb = newbufs(2, "hst")
            it = 0
            for f in range(FC):
                w0, w1 = (f % 2) * 2, (f % 2) * 2 + 1
                wdma(wsl[w0], wgu_d[l * 88 + f], wb[w0])
                wdma(wsl[w1], wgu_d[l * 88 + FC + f], wb[w1])
                hs = f % 2
                for (t0, n) in blocks(To):
                    u = it % 2
                    bg, bu = u * 2, u * 2 + 1
                    it += 1
                    kb.mm(PB[bg], PS[bg][:, 0:n], [(wsl[w0][:, kc, :], hT2[:, kc, t0:t0 + n]) for kc in range(KC)], reads=[wb[w0]])
                    kb.mm(PB[bu], PS[bu][:, 0:n], [(wsl[w1][:, kc, :], hT2[:, kc, t0:t0 + n]) for kc in range(KC)], reads=[wb[w1]])
                    kb.op("act", lambda h, bg=bg, u=u, n=n: h.activation(out=sg[u][:, 0:n], in_=PS[bg][:, 0:n], func=AF.Silu),
                          reads=[PB[bg]], writes=[sgb[u]])
                    kb.op("dve", lambda h, bu=bu, u=u, n=n, t0=t0, hs=hs: h.tensor_tensor(out=hst[hs][:, t0:t0 + n], in0=PS[bu][:, 0:n],
                                                                                        in1=sg[u][:, 0:n], op=ALU.mult),
                          reads=[PB[bu], sgb[u]], writes=[hstb[hs]])
                kb.dma("sp", Hd[f * 128:(f + 1) * 128, 0:To], hst[hs][:, 0:To], reads=[hstb[hs]], key=hstb[hs])
            kb.barrier()

        def phase_ffn_down(l):
            To = t_out(l)
            nt = To // 128
            RA.reset()
            RB.reset()
            MS.reset()
            nsb = 3
            sizes = [nt // nsb + (1 if r < nt % nsb else 0) for r in range(nsb)]
            mx = max(sizes)
            A = RA.alloc(FC * mx * 128 * 2).rearrange("p (f t) -> p f t", t=mx * 128)
            ab = newbufs(4, "Ald")
            wsl = [RB.alloc(FC * 512 * 2).rearrange("p (k t) -> p k t", t=512) for _ in range(2)]
            wb = newbufs(2, "wdn")
            zst = [MS.alloc(512 * 4, F32) for _ in range(3)]
            zstb = newbufs(3, "zst")
            Hv = Hd.rearrange("(f p) t -> p f t", p=128)
            i0 = 0
            for sbi in range(nsb):
                ns = sizes[sbi]
                for q in range(4):
                    kb.dma("sp", A[:, 11 * q:11 * q + 11, 0:ns * 128], Hv[:, 11 * q:11 * q + 11, i0 * 128:(i0 + ns) * 128],
                           writes=[ab[q]], key=ab[q])
                gemm_tm(l, To, A, ab, lambda nb: wdown_d[l * 4 + nb], FC, wsl, wb, zst, zstb,
                        [(i0 + li, li) for li in range(ns)])
                i0 += ns
            kb.barrier()

        phase_mod()
        hT = phase_prep(0)
        for l in range(nlayers):
            v_all, rb_base = phase_v(l, hT)
            phase_qk_attn(l, hT, v_all, rb_base)
            phase_conv_gates(l, hT)
            cbT = phase_convln(l)
            phase_merge(l, cbT)
            phase_wout(l)
            x_in = xin if l == 0 else XB
            hT2 = phase_ln(l, 0, x_in, XA, (l * 64 + 48, l * 64 + 32))
            phase_ffn_up(l, hT2)
            phase_ffn_down(l)
            last = (l == nlayers - 1)
            if last:
                ydst = y if nlayers == L else XB
                phase_ln(l, 1, XA, ydst, None, final=True)
            else:
                hT = phase_ln(l, 1, XA, XB, ((l + 1) * 64 + 16, (l + 1) * 64 + 0))

        with nc.Block() as block:
            def mk(name):
                def run(h):
                    for f in kb.E[name].ops:
                        f(h)
                return run
            block.sync(mk("sp"))
            block.tensor(mk("pe"))
            block.scalar(mk("act"))
            block.vector(mk("dve"))
            block.gpsimd(mk("pool"))
        stats = {n: len(e.ops) for n, e in kb.E.items()}
    return nc, stats


def _fm_into(out, W):
    K, N = W.shape
    out.reshape(N // 128, 128, K // 128, 128)[...] = W.reshape(K // 128, 128, N // 128, 128).transpose(2, 1, 0, 3)


def _tm_into(out, W):
    K, N = W.shape
    out.reshape(N // 512, 128, K // 128, 512)[...] = W.reshape(K // 128, 128, N // 512, 512).transpose(2, 1, 0, 3)


_HPERM = np.concatenate([np.arange(0, 16), np.arange(32, 48), np.arange(16, 32), np.arange(48, 64), np.arange(64, 128)])


def prepare_shared(inp):
    f = lambda a: np.asarray(a, dtype=np.float32)
    sh = {}
    w_ada, w_in = f(inp["w_ada"]), f(inp["w_in"])
    w_oa, w_ob, w_out = f(inp["w_oa"]), f(inp["w_ob"]), f(inp["w_out"])
    w_gu, w_down = f(inp["w_gu"]), f(inp["w_down"])
    sh["wada"] = np.empty((L * 24, 128, KC * 512), np.float32)
    sh["winfm"] = np.empty((L * 84, 128, KC * 128), np.float32)
    sh["wv"] = np.empty((L, 128, KC * 512), np.float32)
    sh["woa"] = np.empty((L * 16, 128, KC * 128), np.float32)
    sh["wob"] = np.empty((L * 16, 128, KC * 128), np.float32)
    sh["wout"] = np.empty((L * 4, 128, KC * 512), np.float32)
    sh["wgu"] = np.empty((L * 88, 128, KC * 128), np.float32)
    sh["wdown"] = np.empty((L * 4, 128, FC * 512), np.float32)
    qkperm = np.concatenate([h * 128 + _HPERM for h in range(NH + NKV)])
    for l in range(L):
        _tm_into(sh["wada"][l * 24:(l + 1) * 24], w_ada[l])
        W = w_in[l]
        _fm_into(sh["winfm"][l * 84:l * 84 + 20], W[:, qkperm])
        _fm_into(sh["winfm"][l * 84 + 20:(l + 1) * 84], W[:, 3072:])
        _tm_into(sh["wv"][l:l + 1], W[:, 2560:3072])
        _fm_into(sh["woa"][l * 16:(l + 1) * 16], w_oa[l])
        _fm_into(sh["wob"][l * 16:(l + 1) * 16], w_ob[l])
        _tm_into(sh["wout"][l * 4:(l + 1) * 4], w_out[l])
        _fm_into(sh["wgu"][l * 88:(l + 1) * 88], w_gu[l])
        _tm_into(sh["wdown"][l * 4:(l + 1) * 4], w_down[l])
    sh["bada"] = f(inp["b_ada"])
    sh["badapp"] = np.ascontiguousarray(f(inp["b_ada"]).reshape(L, 6, 16, 128).transpose(0, 3, 1, 2)).reshape(L, 128, 96)
    sh["sinkb"] = np.ascontiguousarray(np.broadcast_to(f(inp["sink"]).reshape(1, L * NH), (128, L * NH)))
    cg = f(inp["conv_ln_g"]).reshape(L, 16, 128).transpose(0, 2, 1)
    cb = f(inp["conv_ln_b"]).reshape(L, 16, 128).transpose(0, 2, 1)
    sh["clnpp"] = np.ascontiguousarray(np.concatenate([cg, cb], axis=2))
    sh["lnrows"] = np.ascontiguousarray(np.stack([f(inp["ln1_g"]), f(inp["ln1_b"]), f(inp["ln2_g"]), f(inp["ln2_b"])], axis=1))
    sh["ident"] = np.eye(128, dtype=np.float32)
    i = np.arange(128)[:, None]
    j = np.arange(128)[None, :]
    m = np.zeros((128, 384), np.float32)
    m[:, 0:128] = np.where(j >= i, 0.0, NEG)
    m[:, 256:384] = np.where(j <= i, 0.0, NEG)
    sh["mask"] = m
    return sh


def prepare_core(inp, core):
    f = lambda a: np.asarray(a, dtype=np.float32)
    b, flip = core // 2, core % 2
    x = f(inp["x"])[b]
    S = x.shape[0]
    if flip:
        xin = np.ascontiguousarray(x[::-1][0:TMAX])
        pos = (S - 1 - np.arange(TMAX)).astype(np.float32)
    else:
        xin = np.ascontiguousarray(x[0:TMAX])
        pos = np.arange(TMAX).astype(np.float32)
    inv_freq = (np.float32(500000.0) ** (-np.arange(0, 32, 2, dtype=np.float32) / np.float32(32))).astype(np.float32)
    ang = pos[None, :] * inv_freq[:, None]
    cos, sin = np.cos(ang).astype(np.float32), np.sin(ang).astype(np.float32)
    ctab = np.ones((64, TMAX), np.float32)
    stab = np.zeros((64, TMAX), np.float32)
    ctab[0:16] = cos
    ctab[32:48] = cos
    stab[0:16] = -sin
    stab[32:48] = sin
    wd = f(inp["w_dw"])
    if flip:
        wd = wd[:, ::-1, :]
    wdw = np.ascontiguousarray(wd.reshape(L, CW, 16, 128).transpose(0, 3, 2, 1)).reshape(L, 128, 16 * CW)
    cpp = np.ascontiguousarray(f(inp["c"])[b].reshape(16, 128).T)
    return {"xin": xin, "cpp": cpp, "ctab": ctab, "stab": stab, "wdw": wdw}


_CACHE = {}


def kernel(**inputs):
    if "nc" not in _CACHE:
        _CACHE["nc"] = build_program()[0]
    nc = _CACHE["nc"]
    sh = prepare_shared(inputs)
    in_maps = []
    for core in range(8):
        m = dict(sh)
        m.update(prepare_core(inputs, core))
        in_maps.append(m)
    res = run_bass_kernel_spmd(nc, in_maps, core_ids=list(range(8)))
    x = np.asarray(inputs["x"])
    out = np.empty(x.shape, np.float32)
    S = x.shape[1]
    for core in range(8):
        b, flip = core // 2, core % 2
        yv = np.asarray(res.results[core]["y"], dtype=np.float32)
        if flip:
            out[b, S - NOWN:] = yv[::-1]
        else:
            out[b, 0:NOWN] = yv
    return out
```
